# Optimizing a Trainium2 kernel written in Bass

```python
import math
import jax, jax.numpy as jnp
from jax import lax
import numpy as np

D_MODEL = 2048
BATCH = 4
SEQ = 2048
DEPTH = 2
DEC_BATCH = 8
DEC_SEQ = 4
PAST_LEN = 16384
PAGE_SIZE = 128

HEAD_DIM = 128
N_HEADS = D_MODEL // HEAD_DIM
N_MIXERS = 2
N_LAYERS_A = (DEPTH + N_MIXERS - 1) // N_MIXERS
N_LAYERS_B = DEPTH // N_MIXERS
DIL_PAIRS = ((128, 1), (512, 4), (2048, 16))
N_DIL = len(DIL_PAIRS)
BAND_BLOCK = 128
NSA_KV_HEADS = 4
NSA_GROUP = N_HEADS // NSA_KV_HEADS
CMP_BLOCK = 32
CMP_STRIDE = 16
CMP_SPAN = CMP_BLOCK // CMP_STRIDE
CMP_HIDDEN = 128
SEL_BLOCK = 64
SEL_TOPN = 16
NSA_WINDOW = 512
SEL_Q_CHUNK = 32
N_NSA_KV = 6
D_FF = 5632
CONV_W = 3
PLE_DIM = 256
REL_BUCKETS = 32
REL_MAX_DIST = 2048
EPS = 1e-6
NEG = -1e30
FORCED_SCORE = 1e4
SCALE = HEAD_DIM ** -0.5
QKV_A = N_DIL * 3 * N_HEADS * HEAD_DIM
NSA_Q = N_HEADS * HEAD_DIM
NSA_KV = NSA_KV_HEADS * HEAD_DIM
PROJ_B = NSA_Q + N_NSA_KV * NSA_KV + 3 * N_HEADS

kernel_name = 'dilated_nsa_convffn_hybrid_step'


def rmsnorm(x, g):
    xf = x.astype(jnp.float32)
    y = xf * lax.rsqrt(jnp.mean(xf * xf, axis=-1, keepdims=True) + EPS)
    return (y * g.astype(jnp.float32)).astype(x.dtype)


def rel_bucket(dist):
    dist = jnp.maximum(dist, 0)
    max_exact = REL_BUCKETS // 2
    large = max_exact + (jnp.log(jnp.maximum(dist, 1).astype(jnp.float32) / max_exact)
                         / math.log(REL_MAX_DIST / max_exact) * (REL_BUCKETS - max_exact)).astype(jnp.int32)
    large = jnp.minimum(large, REL_BUCKETS - 1)
    return jnp.where(dist < max_exact, dist, large)


def masked_softmax_lse(s, mask):
    s = jnp.where(mask, s, NEG)
    m = jnp.max(s, axis=-1, keepdims=True)
    p = jnp.exp(s - m) * mask
    l = jnp.sum(p, axis=-1, keepdims=True)
    lc = jnp.maximum(l, 1e-30)
    return p / lc, (m + jnp.log(lc))[..., 0]


def band_bias(rel_bias, dist_scale, n_prev, kh, g):
    nk = (n_prev + 1) * BAND_BLOCK
    dist = (jnp.arange(BAND_BLOCK)[:, None] - jnp.arange(nk)[None, :] + n_prev * BAND_BLOCK) * dist_scale
    b = rel_bias[rel_bucket(dist)]
    return b.reshape(BAND_BLOCK, nk, kh, g).transpose(2, 3, 0, 1).astype(jnp.float32)


def banded_attn(q, k, v, bias, n_prev, max_dist):
    N, L = q.shape[:2]
    BB = BAND_BLOCK
    nb = -(-L // BB)
    Lp = nb * BB
    pad = Lp - L
    qb = jnp.pad(q, ((0, 0), (0, pad), (0, 0), (0, 0), (0, 0))).reshape(N, nb, BB, *q.shape[2:])
    kp = jnp.pad(k, ((0, 0), (n_prev * BB, pad), (0, 0), (0, 0)))
    vp = jnp.pad(v, ((0, 0), (n_prev * BB, pad), (0, 0), (0, 0)))

    def blocks(a):
        return jnp.concatenate([a[:, s * BB: s * BB + Lp].reshape(N, nb, BB, *a.shape[2:])
                                for s in range(n_prev + 1)], axis=2)

    kb, vb = blocks(kp), blocks(vp)
    nk = (n_prev + 1) * BB
    s = jnp.einsum('nbqhgd,nbjhd->nbhgqj', qb, kb).astype(jnp.float32) * SCALE + bias
    qi = jnp.arange(BB)[:, None]
    kj = jnp.arange(nk)[None, :]
    dist = qi - kj + n_prev * BB
    kpos = jnp.arange(nb)[:, None, None] * BB + kj[None] - n_prev * BB
    mask = ((dist >= 0) & (dist <= max_dist))[None] & (kpos >= 0)
    p, lse = masked_softmax_lse(s, mask[None, :, None, None])
    o = jnp.einsum('nbhgqj,nbjhd->nbqhgd', p.astype(vb.dtype), vb)
    o = o.reshape(N, Lp, *q.shape[2:])[:, :L]
    lse = lse.transpose(0, 1, 4, 2, 3).reshape(N, Lp, *q.shape[2:4])[:, :L]
    return o, lse


def to_residue(t, dil):
    B, S = t.shape[:2]
    return t.reshape(B, S // dil, dil, *t.shape[2:]).swapaxes(1, 2).reshape(B * dil, S // dil, *t.shape[2:])


def from_residue(t, B, dil):
    Ls = t.shape[1]
    return t.reshape(B, dil, Ls, *t.shape[2:]).swapaxes(1, 2).reshape(B, dil * Ls, *t.shape[2:])


def combine_by_denominator(outs, lses):
    w = jax.nn.softmax(jnp.stack(lses, 0), axis=0)
    o = jnp.einsum('gnth,gnthd->nthd', w, jnp.stack(outs, 0).astype(jnp.float32))
    return o.astype(outs[0].dtype)


def dilated_attn_prompt(a, w_in, w_out, rel_bias):
    B, S, _ = a.shape
    qkv = (a @ w_in).reshape(B, S, N_DIL, 3, N_HEADS, HEAD_DIM)
    outs, lses, new_bufs = [], [], []
    for g, (win, dil) in enumerate(DIL_PAIRS):
        q, k, v = qkv[:, :, g, 0], qkv[:, :, g, 1], qkv[:, :, g, 2]
        band = win // dil
        n_prev = -(-band // BAND_BLOCK)
        bias = band_bias(rel_bias, dil, n_prev, N_HEADS, 1)
        o, lse = banded_attn(to_residue(q, dil)[:, :, :, None], to_residue(k, dil), to_residue(v, dil),
                             bias, n_prev, band)
        outs.append(from_residue(o[:, :, :, 0], B, dil))
        lses.append(from_residue(lse[:, :, :, 0], B, dil))
        keep = min(win, S)
        new_bufs.append(jnp.stack([k, v], axis=2)[:, S - keep:])
    o = combine_by_denominator(outs, lses)
    return o.reshape(B, S, NSA_Q) @ w_out, new_bufs


def dilated_attn_sample(a, bufs, w_in, w_out, rel_bias):
    N, T, _ = a.shape
    qkv = (a @ w_in).reshape(N, T, N_DIL, 3, N_HEADS, HEAD_DIM)
    outs, lses, new_bufs = [], [], []
    for g, (win, dil) in enumerate(DIL_PAIRS):
        buf = bufs[g]
        lb = buf.shape[1]
        q, k, v = qkv[:, :, g, 0], qkv[:, :, g, 1], qkv[:, :, g, 2]
        kk = jnp.concatenate([buf[:, :, 0].astype(k.dtype), k], axis=1)
        vv = jnp.concatenate([buf[:, :, 1].astype(v.dtype), v], axis=1)
        j = jnp.arange(win // dil + 1)
        idx = lb + jnp.arange(T)[:, None] - dil * j[None, :]
        valid = idx >= 0
        idx = jnp.maximum(idx, 0)
        kg, vg = kk[:, idx], vv[:, idx]
        bias = rel_bias[rel_bucket(dil * j)].T.astype(jnp.float32)
        s = jnp.einsum('nthd,ntjhd->nthj', q, kg).astype(jnp.float32) * SCALE + bias
        p, lse = masked_softmax_lse(s, valid[None, :, None, :])
        outs.append(jnp.einsum('nthj,ntjhd->nthd', p.astype(vg.dtype), vg))
        lses.append(lse)
        new_bufs.append(jnp.concatenate([buf, jnp.stack([k, v], axis=2).astype(buf.dtype)], axis=1)[:, T:])
    o = combine_by_denominator(outs, lses)
    return o.reshape(N, T, NSA_Q) @ w_out, new_bufs


def nsa_project(a, w_in):
    N, T, _ = a.shape
    pr = a @ w_in
    q = pr[..., :NSA_Q].reshape(N, T, NSA_KV_HEADS, NSA_GROUP, HEAD_DIM)
    kv = pr[..., NSA_Q:NSA_Q + N_NSA_KV * NSA_KV].reshape(N, T, N_NSA_KV, NSA_KV_HEADS, HEAD_DIM)
    gates = jax.nn.sigmoid(pr[..., NSA_Q + N_NSA_KV * NSA_KV:].astype(jnp.float32))
    return q, kv, gates.reshape(N, T, 3, NSA_KV_HEADS, NSA_GROUP)


def compress(x, pe, w1, b1, w2, b2):
    N, L = x.shape[:2]
    nch = L // CMP_STRIDE
    nc = nch - CMP_SPAN + 1
    ch = x[:, :nch * CMP_STRIDE].reshape(N, nch, CMP_STRIDE, *x.shape[2:])
    pe_r = pe.reshape(CMP_SPAN, CMP_STRIDE, 1, HEAD_DIM).astype(x.dtype)
    w1_r = w1.reshape(CMP_SPAN, CMP_STRIDE, HEAD_DIM, CMP_HIDDEN)
    h = b1
    for s in range(CMP_SPAN):
        h = h + jnp.einsum('ncshd,sdf->nchf', ch[:, s:s + nc] + pe_r[s], w1_r[s])
    return jax.nn.gelu(h) @ w2 + b2


def sel_blocks(x):
    N, L = x.shape[:2]
    nsb = -(-L // SEL_BLOCK)
    xp = jnp.pad(x, ((0, 0), (0, nsb * SEL_BLOCK - L), (0, 0), (0, 0)))
    return xp.reshape(N, nsb, SEL_BLOCK, *x.shape[2:]).transpose(0, 3, 1, 2, 4)


def nsa_cmp_sel(q, qpos, kc, vc, ksb, vsb, table):
    N, T = q.shape[:2]
    nc = kc.shape[1]
    nsb = ksb.shape[2]
    cend = jnp.arange(nc) * CMP_STRIDE + (CMP_BLOCK - 1)
    dist_c = qpos[:, None] - cend[None, :]
    bias_c = table[rel_bucket(dist_c)].transpose(0, 2, 3, 1)
    s = jnp.einsum('nthgd,nchd->nthgc', q, kc).astype(jnp.float32) * SCALE + bias_c[None]
    p_cmp, _ = masked_softmax_lse(s, (dist_c >= 0)[None, :, None, None, :])
    o_cmp = jnp.einsum('nthgc,nchd->nthgd', p_cmp.astype(vc.dtype), vc)
    pc = jnp.sum(p_cmp, axis=3)
    ratio = SEL_BLOCK // CMP_STRIDE
    right = max(0, ratio * nsb - nc)
    pcp = jnp.pad(pc, ((0, 0), (0, 0), (0, 0), (CMP_SPAN - 1, right)))
    p_slc = 0.0
    for m in range(ratio):
        for n in range(CMP_SPAN):
            o = m + n
            p_slc = p_slc + pcp[..., o:o + ratio * (nsb - 1) + 1:ratio]
    jb = jnp.arange(nsb)[None, :]
    cur = (qpos // SEL_BLOCK)[:, None]
    valid = jb <= cur
    forced = (jb == 0) | (jb == cur) | (jb == cur - 1)
    score = jnp.where(forced[None, :, None, :], FORCED_SCORE,
                      jnp.where(valid[None, :, None, :], p_slc, -1.0))
    n_sel = min(SEL_TOPN, nsb)
    _, idx = lax.top_k(score, n_sel)
    ni = jnp.arange(N)[:, None, None, None]
    hi = jnp.arange(NSA_KV_HEADS)[None, None, :, None]
    kg, vg = ksb[ni, hi, idx], vsb[ni, hi, idx]
    kpos = idx[..., None] * SEL_BLOCK + jnp.arange(SEL_BLOCK)
    dist_s = qpos[None, :, None, None, None] - kpos
    bias_s = table[rel_bucket(dist_s), hi[..., None]].transpose(0, 1, 2, 5, 3, 4)
    s = jnp.einsum('nthgd,nthjsd->nthgjs', q, kg).astype(jnp.float32) * SCALE + bias_s
    nk = n_sel * SEL_BLOCK
    p, _ = masked_softmax_lse(s.reshape(N, T, NSA_KV_HEADS, NSA_GROUP, nk),
                              (dist_s >= 0).reshape(N, T, NSA_KV_HEADS, 1, nk))
    o_sel = jnp.einsum('nthgj,nthjd->nthgd', p.astype(vg.dtype), vg.reshape(N, T, NSA_KV_HEADS, nk, HEAD_DIM))
    return o_cmp, o_sel


def nsa_merge(a, gates, o_cmp, o_sel, o_win, w_out):
    N, T = a.shape[:2]
    o = (gates[:, :, 0, :, :, None] * o_cmp.astype(jnp.float32)
         + gates[:, :, 1, :, :, None] * o_sel.astype(jnp.float32)
         + gates[:, :, 2, :, :, None] * o_win.astype(jnp.float32)).astype(a.dtype)
    return o.reshape(N, T, NSA_Q) @ w_out


def nsa_prompt(a, w_in, w_out, cmp, rel_bias):
    B, S, _ = a.shape
    pe, w1, b1, w2, b2 = cmp
    q, kv, gates = nsa_project(a, w_in)
    table = rel_bias.reshape(REL_BUCKETS, NSA_KV_HEADS, NSA_GROUP).astype(jnp.float32)
    kc = compress(kv[:, :, 0], pe[0], w1[0], b1[0], w2[0], b2[0])
    vc = compress(kv[:, :, 1], pe[1], w1[1], b1[1], w2[1], b2[1])
    ksb, vsb = sel_blocks(kv[:, :, 2]), sel_blocks(kv[:, :, 3])
    nq = S // SEL_Q_CHUNK
    qs = q.reshape(B, nq, SEL_Q_CHUNK, NSA_KV_HEADS, NSA_GROUP, HEAD_DIM).swapaxes(0, 1)
    ps = jnp.arange(S, dtype=jnp.int32).reshape(nq, SEL_Q_CHUNK)

    def body(xs):
        return nsa_cmp_sel(xs[0], xs[1], kc, vc, ksb, vsb, table)

    o_cmp, o_sel = lax.map(body, (qs, ps))
    o_cmp = o_cmp.swapaxes(0, 1).reshape(B, S, NSA_KV_HEADS, NSA_GROUP, HEAD_DIM)
    o_sel = o_sel.swapaxes(0, 1).reshape(B, S, NSA_KV_HEADS, NSA_GROUP, HEAD_DIM)
    n_prev = -(-(NSA_WINDOW - 1) // BAND_BLOCK)
    bias = band_bias(rel_bias, 1, n_prev, NSA_KV_HEADS, NSA_GROUP)
    o_win, _ = banded_attn(q, kv[:, :, 4], kv[:, :, 5], bias, n_prev, NSA_WINDOW - 1)
    y = nsa_merge(a, gates, o_cmp, o_sel, o_win, w_out)
    keep = min(NSA_WINDOW, S)
    return y, kv[:, S - keep:, 4:], kv[:, :, :4]


def nsa_sample(a, pool, page_table, win_buf, w_in, w_out, cmp, rel_bias):
    N, T, _ = a.shape
    pe, w1, b1, w2, b2 = cmp
    q, kv, gates = nsa_project(a, w_in)
    table = rel_bias.reshape(REL_BUCKETS, NSA_KV_HEADS, NSA_GROUP).astype(jnp.float32)
    past = pool[page_table]
    past = past.reshape(N, -1, *past.shape[3:]).astype(kv.dtype)
    P = past.shape[1]
    full = jnp.concatenate([past, kv[:, :, :4]], axis=1)
    kc = compress(full[:, :, 0], pe[0], w1[0], b1[0], w2[0], b2[0])
    vc = compress(full[:, :, 1], pe[1], w1[1], b1[1], w2[1], b2[1])
    ksb, vsb = sel_blocks(full[:, :, 2]), sel_blocks(full[:, :, 3])
    qpos = P + jnp.arange(T, dtype=jnp.int32)
    o_cmp, o_sel = nsa_cmp_sel(q, qpos, kc, vc, ksb, vsb, table)
    lw = win_buf.shape[1]
    kw = jnp.concatenate([win_buf[:, :, 0].astype(kv.dtype), kv[:, :, 4]], axis=1)
    vw = jnp.concatenate([win_buf[:, :, 1].astype(kv.dtype), kv[:, :, 5]], axis=1)
    kpos = P - lw + jnp.arange(lw + T)
    dist = qpos[:, None] - kpos[None, :]
    mask = (dist >= 0) & (dist <= NSA_WINDOW - 1)
    bias = table[rel_bucket(dist)].transpose(2, 3, 0, 1)
    s = jnp.einsum('nthgd,njhd->nhgtj', q, kw).astype(jnp.float32) * SCALE + bias
    p, _ = masked_softmax_lse(s, mask)
    o_win = jnp.einsum('nhgtj,njhd->nthgd', p.astype(vw.dtype), vw)
    y = nsa_merge(a, gates, o_cmp, o_sel, o_win, w_out)
    new_win = jnp.concatenate([win_buf, kv[:, :, 4:].astype(win_buf.dtype)], axis=1)[:, T:]
    return y, new_win, kv[:, :, :4]


def conv_ffn(f, conv_prev, w_in, conv_w, conv_b, w_out):
    T = f.shape[1]
    hu = f @ w_in
    gate, val = hu[..., :D_FF], hu[..., D_FF:]
    ext = jnp.concatenate([conv_prev.astype(gate.dtype), gate], axis=1)
    c = conv_b
    for j in range(CONV_W):
        c = c + ext[:, j:j + T] * conv_w[j]
    return (jax.nn.gelu(c) * val) @ w_out, ext[:, T:]


def ple_add(h, p, g_norm, w_gate, w_proj):
    return h + jax.nn.sigmoid(rmsnorm(h, g_norm) @ w_gate) * (p.astype(h.dtype) @ w_proj)


def setup_inputs(seed: int = 0) -> dict:
    key = jax.random.key(seed)
    ks = iter(jax.random.split(key, 48))

    def nrm(shape, scale):
        return scale * jax.random.normal(next(ks), shape, jnp.float32)

    n_pages = PAST_LEN // PAGE_SIZE
    n_phys = (5 * DEC_BATCH * n_pages) // 4
    page_table = jax.random.permutation(next(ks), n_phys)[:DEC_BATCH * n_pages]
    page_table = page_table.reshape(DEC_BATCH, n_pages).astype(jnp.int32)
    dil_len = [min(w, PAST_LEN) for (w, _) in DIL_PAIRS]
    return {
        'x_prompt': nrm((BATCH, SEQ, D_MODEL), 1.0),
        'x_sample': nrm((DEC_BATCH, DEC_SEQ, D_MODEL), 1.0),
        'state_dil_w128': nrm((N_LAYERS_A, DEC_BATCH, dil_len[0], 2, N_HEADS, HEAD_DIM), 1.0),
        'state_dil_w512': nrm((N_LAYERS_A, DEC_BATCH, dil_len[1], 2, N_HEADS, HEAD_DIM), 1.0),
        'state_dil_w2048': nrm((N_LAYERS_A, DEC_BATCH, dil_len[2], 2, N_HEADS, HEAD_DIM), 1.0),
        'state_nsa_win': nrm((N_LAYERS_B, DEC_BATCH, min(NSA_WINDOW, PAST_LEN), 2, NSA_KV_HEADS, HEAD_DIM), 1.0),
        'state_conv': nrm((DEPTH, DEC_BATCH, CONV_W - 1, D_FF), 1.0),
        'cache_nsa_kv': nrm((N_LAYERS_B, n_phys, PAGE_SIZE, 4, NSA_KV_HEADS, HEAD_DIM), 1.0),
        'page_table': page_table,
        'p_prompt': nrm((DEPTH, BATCH, SEQ, PLE_DIM), 1.0),
        'p_sample': nrm((DEPTH, DEC_BATCH, DEC_SEQ, PLE_DIM), 1.0),
        'rel_bias': nrm((REL_BUCKETS, N_HEADS), 0.5),
        'norm_mix': 1.0 + nrm((DEPTH, D_MODEL), 0.1),
        'norm_ffn': 1.0 + nrm((DEPTH, D_MODEL), 0.1),
        'norm_ple': 1.0 + nrm((DEPTH, D_MODEL), 0.1),
        'norm_final': 1.0 + nrm((D_MODEL,), 0.1),
        'w_in_a': nrm((N_LAYERS_A, D_MODEL, QKV_A), D_MODEL ** -0.5),
        'w_out_a': nrm((N_LAYERS_A, NSA_Q, D_MODEL), NSA_Q ** -0.5),
        'w_in_b': nrm((N_LAYERS_B, D_MODEL, PROJ_B), D_MODEL ** -0.5),
        'w_out_b': nrm((N_LAYERS_B, NSA_Q, D_MODEL), NSA_Q ** -0.5),
        'cmp_pe': nrm((N_LAYERS_B, 2, CMP_BLOCK, HEAD_DIM), 0.2),
        'cmp_w1': nrm((N_LAYERS_B, 2, CMP_BLOCK, HEAD_DIM, CMP_HIDDEN), (CMP_BLOCK * HEAD_DIM) ** -0.5),
        'cmp_b1': nrm((N_LAYERS_B, 2, CMP_HIDDEN), 0.02),
        'cmp_w2': nrm((N_LAYERS_B, 2, CMP_HIDDEN, HEAD_DIM), CMP_HIDDEN ** -0.5),
        'cmp_b2': nrm((N_LAYERS_B, 2, HEAD_DIM), 0.02),
        'w_ffn_in': nrm((DEPTH, D_MODEL, 2 * D_FF), D_MODEL ** -0.5),
        'conv_w': nrm((DEPTH, CONV_W, D_FF), CONV_W ** -0.5),
        'conv_b': nrm((DEPTH, D_FF), 0.02),
        'w_ffn_out': nrm((DEPTH, D_FF, D_MODEL), D_FF ** -0.5),
        'w_ple_gate': nrm((DEPTH, D_MODEL, D_MODEL), D_MODEL ** -0.5),
        'w_ple_proj': nrm((DEPTH, PLE_DIM, D_MODEL), PLE_DIM ** -0.5),
    }


def reference(x_prompt, x_sample, state_dil_w128, state_dil_w512, state_dil_w2048, state_nsa_win,
              state_conv, cache_nsa_kv, page_table, p_prompt, p_sample, rel_bias, norm_mix, norm_ffn,
              norm_ple, norm_final, w_in_a, w_out_a, w_in_b, w_out_b, cmp_pe, cmp_w1, cmp_b1, cmp_w2,
              cmp_b2, w_ffn_in, conv_w, conv_b, w_ffn_out, w_ple_gate, w_ple_proj):
    hp, hs = x_prompt, x_sample
    B = x_prompt.shape[0]
    dil_p = [[] for _ in range(N_DIL)]
    dil_s = [[] for _ in range(N_DIL)]
    win_p, win_s, kv_p, kv_s, conv_p, conv_s = [], [], [], [], [], []
    for i in range(DEPTH):
        li = i // N_MIXERS
        ap, asm = rmsnorm(hp, norm_mix[i]), rmsnorm(hs, norm_mix[i])
        if i % N_MIXERS == 0:
            yp, bp = dilated_attn_prompt(ap, w_in_a[li], w_out_a[li], rel_bias)
            ys, bs = dilated_attn_sample(asm, (state_dil_w128[li], state_dil_w512[li], state_dil_w2048[li]),
                                         w_in_a[li], w_out_a[li], rel_bias)
            for g in range(N_DIL):
                dil_p[g].append(bp[g])
                dil_s[g].append(bs[g])
        else:
            cmp = (cmp_pe[li], cmp_w1[li], cmp_b1[li], cmp_w2[li], cmp_b2[li])
            yp, wp, rp = nsa_prompt(ap, w_in_b[li], w_out_b[li], cmp, rel_bias)
            ys, ws, rs = nsa_sample(asm, cache_nsa_kv[li], page_table, state_nsa_win[li],
                                    w_in_b[li], w_out_b[li], cmp, rel_bias)
            win_p.append(wp)
            win_s.append(ws)
            kv_p.append(rp)
            kv_s.append(rs)
        hp, hs = hp + yp, hs + ys
        zeros_prev = jnp.zeros((B, CONV_W - 1, D_FF), hp.dtype)
        fp, cp = conv_ffn(rmsnorm(hp, norm_ffn[i]), zeros_prev, w_ffn_in[i], conv_w[i], conv_b[i], w_ffn_out[i])
        fs, cs = conv_ffn(rmsnorm(hs, norm_ffn[i]), state_conv[i], w_ffn_in[i], conv_w[i], conv_b[i], w_ffn_out[i])
        conv_p.append(cp)
        conv_s.append(cs)
        hp, hs = hp + fp, hs + fs
        hp = ple_add(hp, p_prompt[i], norm_ple[i], w_ple_gate[i], w_ple_proj[i])
        hs = ple_add(hs, p_sample[i], norm_ple[i], w_ple_gate[i], w_ple_proj[i])
    y_prompt = rmsnorm(hp, norm_final)
    y_sample = rmsnorm(hs, norm_final)
    return (y_prompt, y_sample,
            jnp.stack(dil_p[0]), jnp.stack(dil_s[0]),
            jnp.stack(dil_p[1]), jnp.stack(dil_s[1]),
            jnp.stack(dil_p[2]), jnp.stack(dil_s[2]),
            jnp.stack(win_p), jnp.stack(win_s),
            jnp.stack(conv_p), jnp.stack(conv_s),
            jnp.stack(kv_p), jnp.stack(kv_s))
```

```python
import math
import os
from contextlib import ExitStack
import numpy as np
import ml_dtypes
import concourse.bass as bass
import concourse.mybir as mybir
from concourse.bass_utils import run_bass_kernel_spmd
F32 = mybir.dt.float32
BF16 = mybir.dt.bfloat16
I32 = mybir.dt.int32
U32 = mybir.dt.uint32
AF = mybir.ActivationFunctionType
ALU = mybir.AluOpType
AX = mybir.AxisListType

NDMA_SEM = 6
SAME_ENGINE_SYNC = True


def _region(ap):
    t = ap.tensor
    shape = list(t.shape)
    rowsize = 1
    for s in shape[1:]:
        rowsize *= s
    nrows = shape[0]
    if t.name.startswith("ps"):
        return (t.name, 0, nrows, 0, rowsize)
    off = ap.offset
    r0 = off // rowsize
    c0 = off % rowsize
    rlo = rhi = r0
    clo = chi = c0
    for step, cnt in ap.ap:
        if cnt <= 1 or step == 0:
            continue
        ext = step * (cnt - 1)
        if step % rowsize == 0:
            e = ext // rowsize
            if e > 0:
                rhi += e
            else:
                rlo += e
        else:
            if ext > 0:
                chi += ext
            else:
                clo += ext
    if clo < 0 or chi >= rowsize:
        return (t.name, 0, nrows, 0, rowsize)
    return (t.name, rlo, rhi + 1, clo, chi + 1)


def _overlap(a, b):
    return a[1] < b[2] and b[1] < a[2] and a[3] < b[4] and b[3] < a[4]


def _contains(a, b):
    return a[1] <= b[1] and b[2] <= a[2] and a[3] <= b[3] and b[4] <= a[4]


class Op:
    __slots__ = ("idx", "eng", "fn", "is_dma", "deps", "signal", "sem", "semval", "waits", "nosync_same")

    def __init__(self, idx, eng, fn, is_dma):
        self.idx = idx
        self.eng = eng
        self.fn = fn
        self.is_dma = is_dma
        self.deps = set()
        self.signal = False
        self.sem = None
        self.semval = 0
        self.waits = []
        self.nosync_same = False


class Prog:
    ENGS = ("sync", "act", "dve", "pool", "pe")

    def __init__(self, nc, serial=False):
        self.nc = nc
        self.ops = []
        self.acc = {}
        self.serial = serial

    def _track(self, op, reads, writes):
        for ap in reads:
            reg = _region(ap)
            lst = self.acc.setdefault(reg[0], [])
            is_ps = reg[0].startswith("ps")
            for (r, i, w) in lst:
                if w and _overlap(r, reg):
                    op.deps.add(i)
                elif is_ps and not w and self.ops[i].eng != op.eng:
                    op.deps.add(i)
        for ap in writes:
            reg = _region(ap)
            lst = self.acc.setdefault(reg[0], [])
            for (r, i, w) in lst:
                if _overlap(r, reg):
                    op.deps.add(i)
        for ap in reads:
            reg = _region(ap)
            self.acc[reg[0]].append((reg, op.idx, False))
        for ap in writes:
            reg = _region(ap)
            lst = self.acc[reg[0]]
            lst[:] = [x for x in lst if not _contains(reg, x[0])]
            lst.append((reg, op.idx, True))
        op.deps.discard(op.idx)
        best = {}
        keep = set()
        for i in op.deps:
            d = self.ops[i]
            if d.is_dma:
                keep.add(i)
            elif best.get(d.eng, -1) < i:
                best[d.eng] = i
        keep.update(best.values())
        op.deps = keep

    def op(self, eng, fn, reads=(), writes=(), pe_acc=False):
        o = Op(len(self.ops), eng, fn, False)
        o.nosync_same = pe_acc
        self.ops.append(o)
        self._track(o, reads, writes)
        return o

    def dma(self, eng, out, in_, extra_reads=(), **kw):
        def fn(e, out=out, in_=in_, kw=kw):
            return e.dma_start(out=out, in_=in_, **kw)
        o = Op(len(self.ops), eng, fn, True)
        self.ops.append(o)
        self._track(o, [in_] + list(extra_reads), [out])
        return o

    def dma_custom(self, eng, fn, reads, writes):
        o = Op(len(self.ops), eng, fn, True)
        self.ops.append(o)
        self._track(o, reads, writes)
        return o

    def finalize(self, stack):
        nc = self.nc
        ops = self.ops
        if self.serial:
            for i, o in enumerate(ops):
                o.deps = {i - 1} if i > 0 else set()
        for o in ops:
            for d in o.deps:
                dop = ops[d]
                if dop.is_dma:
                    continue
                if dop.eng == o.eng and (dop.eng == "pe" or not SAME_ENGINE_SYNC):
                    continue
                dop.signal = True
        csem = {e: stack.enter_context(nc.semaphore("c_" + e)) for e in self.ENGS}
        dsem = {e: [stack.enter_context(nc.semaphore("d_%s%d" % (e, k))) for k in range(NDMA_SEM)]
                for e in ("sync", "pool", "act")}
        ccount = {e: 0 for e in self.ENGS}
        dcount = {e: 0 for e in dsem}
        seen = {e: {} for e in self.ENGS}
        semobj = {}
        last_dma_vals = {}
        for o in ops:
            waits = {}

            def need(sem, val):
                k = id(sem)
                semobj[k] = sem
                if seen[o.eng].get(k, 0) >= val:
                    return
                if waits.get(k, 0) < val:
                    waits[k] = val
            for d in sorted(o.deps):
                dop = ops[d]
                if dop.is_dma:
                    need(dop.sem, dop.semval)
                else:
                    if not dop.signal:
                        continue
                    if dop.eng == o.eng and (dop.eng == "pe" or not SAME_ENGINE_SYNC):
                        continue
                    need(dop.sem, dop.semval)
            if o.is_dma:
                n = dcount[o.eng]
                dcount[o.eng] += 1
                k = n % NDMA_SEM
                o.sem = dsem[o.eng][k]
                o.semval = 16 * (n // NDMA_SEM + 1)
                if n >= NDMA_SEM:
                    need(o.sem, o.semval - 16)
                last_dma_vals[id(o.sem)] = (o.sem, o.semval)
            elif o.signal:
                ccount[o.eng] += 1
                o.sem = csem[o.eng]
                o.semval = ccount[o.eng]
            for k, v in waits.items():
                seen[o.eng][k] = v
            o.waits = [(semobj[k], v) for k, v in waits.items()]
        self.final_waits = list(last_dma_vals.values())
        self.nwaits = sum(len(o.waits) for o in ops)
        self.counts = (dict(ccount), dict(dcount), len(ops))

    def emit(self, block):
        ops = self.ops
        final_waits = self.final_waits

        def run(eng_name, e):
            for o in ops:
                if o.eng != eng_name:
                    continue
                for (s, v) in o.waits:
                    e.wait_ge(s, v)
                ins = o.fn(e)
                if o.is_dma:
                    ins.then_inc(o.sem, 16)
                elif o.signal:
                    ins.then_inc(o.sem, 1)
            if eng_name == "sync":
                for (s, v) in final_waits:
                    e.wait_ge(s, v)

        @block.sync
        def _(e):
            run("sync", e)

        @block.scalar
        def _(e):
            run("act", e)

        @block.vector
        def _(e):
            run("dve", e)

        @block.gpsimd
        def _(e):
            run("pool", e)

        @block.tensor
        def _(e):
            run("pe", e)

S = 2048
D = 2048
NT = 2052
H = 16
DH = 128
DFF = 5632
PLE = 256
SCALE = float(DH ** -0.5)
NEG = -30000.0
EPS = 1e-6
DIL = ((128, 1), (512, 4), (2048, 16))
PAST = 16384
NPAGE = 128
NSB_S = 257
TILES = [(i, i * 128, 128) for i in range(16)] + [(16, 2048, 4)]


def _bucket(dist):
    dist = np.maximum(np.asarray(dist, np.int64), 0)
    me = 16
    df = np.maximum(dist, 1).astype(np.float32)
    large = me + (np.log(df / np.float32(me)) / np.float32(math.log(2048 / me)) * np.float32(32 - me)).astype(np.int32)
    large = np.minimum(large, 31)
    return np.where(dist < me, dist, large).astype(np.int64)


def _onehot(dist, valid):
    dist = np.asarray(dist).reshape(-1)
    valid = np.asarray(valid).reshape(-1)
    n = dist.shape[0]
    t = np.zeros((33, n), np.float32)
    b = _bucket(dist)
    idx = np.nonzero(valid)[0]
    t[b[idx], idx] = 1.0
    t[32, ~valid] = NEG
    return t


def host_tables():
    tabs = []
    offs = {}
    pos = 0

    def add(name, dist, valid):
        nonlocal pos
        t = _onehot(dist, valid)
        n = t.shape[1]
        npad = (-n) % 512
        if npad:
            t = np.concatenate([t, np.zeros((33, npad), np.float32)], axis=1)
        offs[name] = (pos, n)
        tabs.append(t)
        pos += t.shape[1]
    for g, (win, dil) in enumerate(DIL):
        u = np.arange(383)
        d = 255 - u
        add("l0p%d" % g, d * dil, (d >= 0) & (d <= 128))
    t4 = np.arange(4)[:, None]
    i = np.arange(132)[None, :]
    d = 128 + t4 - i
    add("l0s0", d, (d >= 0) & (d <= 128))
    i = np.arange(516)[None, :]
    d = 512 + t4 - i
    add("l0s1", d, (d >= 0) & (d <= 512) & (d % 4 == 0))
    key = np.arange(516)[None, :]
    tp = np.where(key < 512, key // 128, key - 512)
    a = np.where(key < 512, key % 128, 128)
    d = 2048 + t4 - 16 * a - tp
    add("l0s2", d, (tp == t4) & (d >= 0))
    qi = np.arange(128)[:, None]
    z = np.arange(247)[None, :]
    d = qi - 16 * z + 1889
    add("cmpp", d, d >= 0)
    u = np.arange(2175)
    d = 2047 - u
    add("selp", d, d >= 0)
    u = np.arange(767)
    d = 639 - u
    add("winp", d, (d >= 0) & (d <= 511))
    c = np.arange(1024)[None, :]
    d = PAST + t4 - 16 * c - 31
    add("cmps", d, (d >= 0) & (c < 1023))
    k = np.arange(16512)[None, :]
    d = PAST + t4 - k
    add("sels", d, (d >= 0) & (k < PAST + 4))
    i = np.arange(516)[None, :]
    d = 512 + t4 - i
    add("wins", d, (d >= 0) & (d <= 511))
    tab = np.concatenate(tabs, axis=1)
    return tab, offs


def sel_tables():
    t = np.arange(S)[:, None]
    jb = np.arange(32)[None, :]
    cur = t // 64
    valid = jb <= cur
    forced = (jb == 0) | (jb == cur) | (jb == cur - 1)
    A = (valid & ~forced).astype(np.float32)
    Bv = np.where(forced, 1e4, np.where(valid, 0.0, -1.0)).astype(np.float32)
    jb = np.arange(264)[None, :]
    cur = np.full((4, 1), (PAST // 64))
    valid = jb <= cur
    forced = (jb == 0) | (jb == cur) | (jb == cur - 1)
    As = np.broadcast_to((valid & ~forced), (4, 264)).astype(np.float32)
    Bs = np.broadcast_to(np.where(forced, 1e4, np.where(valid, 0.0, -1.0)), (4, 264)).astype(np.float32)
    return A, Bv, np.ascontiguousarray(As), np.ascontiguousarray(Bs)

class Arena:
    def __init__(self, t, n):
        self.t = t
        self.n = n
        self.p = 0

    def reset(self):
        self.p = 0

    def alloc(self, *shape):
        n = 1
        for s in shape:
            n *= s
        n2 = (n + 15) // 16 * 16
        assert self.p + n2 <= self.n, ("arena overflow", self.p, n2, self.n)
        v = self.t[:, self.p:self.p + n]
        self.p += n2
        if len(shape) == 2:
            v = v.rearrange("p (a b) -> p a b", a=shape[0])
        elif len(shape) == 3:
            v = v.rearrange("p (a b c) -> p a b c", a=shape[0], b=shape[1])
        return v


NTAB_COLS = None


UPTO_DEFAULT = 99


def build(upto=None, serial=False):
    if upto is None:
        upto = UPTO_DEFAULT
    tab, toffs = host_tables()
    NTAB = tab.shape[1]
    nc = bass.Bass("TRN2", target_bir_lowering=False)

    def din(name, shape, dt=F32):
        return nc.dram_tensor(name, list(shape), dt, kind="ExternalInput").ap()

    def dout(name, shape, dt=F32):
        return nc.dram_tensor(name, list(shape), dt, kind="ExternalOutput").ap()

    def dscr(name, shape, dt=F32):
        return nc.dram_tensor(name, list(shape), dt, kind="Internal").ap()

    x = din("x", [NT, D])
    pp = din("pp", [2, NT, PLE])
    st128 = din("st128", [128, 2, D])
    st512 = din("st512", [512, 2, D])
    st2048 = din("st2048", [2048, 2, D])
    stwin = din("stwin", [512, 2, 512])
    stconv = din("stconv", [2, 2, DFF])
    cache = din("cache", [(1 if (os.environ.get("KSMALL") or upto < 8) else 1280) * 128, 2048])
    ptab = din("ptab", [1, 128], I32)
    relb = din("relb", [32, 16])
    norms = din("norms", [7, D])
    w_in_a = din("w_in_a", [D, 18432])
    w_out_a = din("w_out_a", [D, D])
    w_in_b = din("w_in_b", [D, 5168])
    w_out_b = din("w_out_b", [D, D])
    cmp_pe = din("cmp_pe", [2, 32, 128])
    cmp_w1 = din("cmp_w1", [2, 32, 128, 128])
    cmp_b1 = din("cmp_b1", [2, 128])
    cmp_w2 = din("cmp_w2", [2, 128, 128])
    cmp_b2 = din("cmp_b2", [2, 128])
    w_ffn_in = din("w_ffn_in", [2, D, 2 * DFF])
    conv_w = din("conv_w", [2, 3, DFF])
    conv_b = din("conv_b", [2, DFF])
    w_ffn_out = din("w_ffn_out", [2, DFF, D])
    w_ple_gate = din("w_ple_gate", [2, D, D])
    w_ple_proj = din("w_ple_proj", [2, PLE, D])
    c_tab = din("c_tab", [33, NTAB])
    c_idf = din("c_idf", [128, 128])
    c_idb = din("c_idb", [128, 128], BF16)
    c_selA = din("c_selA", [S, 32])
    c_selB = din("c_selB", [S, 32])
    c_selAs = din("c_selAs", [4, 264])
    c_selBs = din("c_selBs", [4, 264])
    c_rowv = din("c_rowv", [S, 1])
    c_selm = din("c_selm", [16, 4])
    c_iota = din("c_iota", [128, 1])
    o_y = dout("o_y", [NT, D])
    o_dp = [dout("o_d%dp" % w, [min(w, S), 2, D]) for (w, _) in DIL]
    o_ds = [dout("o_d%ds" % w, [w, 2, D]) for (w, _) in DIL]
    o_winp = dout("o_winp", [512, 2, 512])
    o_wins = dout("o_wins", [512, 2, 512])
    o_convp = dout("o_convp", [2, 2, DFF])
    o_convs = dout("o_convs", [2, 2, DFF])
    o_kvp = dout("o_kvp", [S, 2048])
    o_kvs = dout("o_kvs", [4, 2048])
    HA = dscr("HA", [NT, D])
    HB = dscr("HB", [NT, D])
    QT0 = dscr("QT0", [3 * 16 * 128, NT], BF16)
    KT0 = dscr("KT0", [3 * 16 * 128, NT], BF16)
    V0 = dscr("V0", [3, NT, D], BF16)
    OG = dscr("OG", [3, NT, D])
    ML = dscr("ML", [3, NT, 32])
    UT = dscr("UT", [DFF, NT], BF16)
    TB = dscr("TB", [16, NTAB])
    QT1 = dscr("QT1", [16 * 128, NT], BF16)
    XT1 = dscr("XT1", [8 * 128, NT], BF16)
    KST = dscr("KST", [4 * 128, NT], BF16)
    KWT = dscr("KWT", [4 * 128, NT], BF16)
    VS1 = dscr("VS1", [NT, 512], BF16)
    VW1 = dscr("VW1", [NT, 512], BF16)
    G1 = dscr("G1", [NT, 48])
    OM = dscr("OM", [NT, D], BF16)
    CXT = dscr("CXT", [8 * 128, PAST], BF16)
    CKST = dscr("CKST", [4 * 128, PAST], BF16)
    CVS = dscr("CVS", [PAST, 512], BF16)
    TZL = {"l0p0": 383, "l0p1": 383, "l0p2": 383, "selp": 2175, "winp": 767}
    TZ = {k: dscr("TZ_" + k, [16, 128, L]) for k, L in TZL.items()}

    st = ExitStack()
    NB_ELEMS = 60000
    NF_ELEMS = 13000
    arb_t = st.enter_context(nc.sbuf_tensor("arena_b", [128, NB_ELEMS], BF16))
    arf_t = st.enter_context(nc.sbuf_tensor("arena_f", [128, NF_ELEMS], F32))
    idf = st.enter_context(nc.sbuf_tensor("idf", [128, 128], F32))
    idb = st.enter_context(nc.sbuf_tensor("idb", [128, 128], BF16))
    small = st.enter_context(nc.sbuf_tensor("small", [128, 256], F32))
    rb33 = st.enter_context(nc.sbuf_tensor("rb33", [33, 16], F32))
    pt_i32 = st.enter_context(nc.sbuf_tensor("pt_i32", [128, 128], I32))
    idx_i32 = st.enter_context(nc.sbuf_tensor("idx_i32", [128, 128], I32))
    selm_sb = st.enter_context(nc.sbuf_tensor("selm_sb", [16, 4], F32))
    selmT_sb = st.enter_context(nc.sbuf_tensor("selmT_sb", [4, 16], F32))
    iota_sb = st.enter_context(nc.sbuf_tensor("iota_sb", [128, 1], F32))
    psA = [st.enter_context(nc.psum_tensor("psA%d" % i, [128, 512], F32)) for i in range(4)]
    psT = [st.enter_context(nc.psum_tensor("psT%d" % i, [128, 1024], BF16)) for i in range(2)]
    psO = [st.enter_context(nc.psum_tensor("psO%d" % i, [128, 512], F32)) for i in range(2)]
    AB = Arena(arb_t, NB_ELEMS)
    AFl = Arena(arf_t, NF_ELEMS)
    P = Prog(nc, serial=serial)
    cnt = {"evac": 0, "psA": 0, "psT": 0, "psO": 0, "q": 0}

    def ACT(out, in_, func=AF.Copy, bias=None, scale=None, extra_reads=()):
        kw = {}
        rd = [in_] + list(extra_reads)
        if bias is not None:
            kw["bias"] = bias
            if not isinstance(bias, (int, float)):
                rd.append(bias)
        if scale is not None:
            kw["scale"] = scale
            if not isinstance(scale, (int, float)):
                rd.append(scale)
        return P.op("act", lambda e: e.activation(out=out, in_=in_, func=func, **kw), reads=rd, writes=[out])

    def TT(out, a, b, op, eng="dve"):
        return P.op(eng, lambda e: e.tensor_tensor(out=out, in0=a, in1=b, op=op), reads=[a, b], writes=[out])

    def TS(out, a, s1, op0, s2=None, op1=None, eng="dve"):
        rd = [a]
        if not isinstance(s1, (int, float)):
            rd.append(s1)
        if s2 is not None and not isinstance(s2, (int, float)):
            rd.append(s2)
        if op1 is None:
            return P.op(eng, lambda e: e.tensor_scalar(out=out, in0=a, scalar1=s1, scalar2=None, op0=op0), reads=rd, writes=[out])
        return P.op(eng, lambda e: e.tensor_scalar(out=out, in0=a, scalar1=s1, scalar2=s2, op0=op0, op1=op1), reads=rd, writes=[out])

    def STT(out, a, s, b, op0, op1):
        rd = [a, b]
        if not isinstance(s, (int, float)):
            rd.append(s)
        return P.op("dve", lambda e: e.scalar_tensor_tensor(out=out, in0=a, scalar=s, in1=b, op0=op0, op1=op1), reads=rd, writes=[out])

    def CP(out, in_, eng=None):
        if eng is None:
            cnt["evac"] += 1
            eng = "act" if cnt["evac"] % 2 else "dve"
        if eng == "act":
            return ACT(out, in_)
        return P.op(eng, lambda e: e.tensor_copy(out=out, in_=in_), reads=[in_], writes=[out])

    def RED(out, in_, op, negate=False):
        return P.op("dve", lambda e: e.tensor_reduce(out=out, in_=in_, axis=AX.X, op=op, negate=negate), reads=[in_], writes=[out])

    def RECIP(out, in_):
        return P.op("dve", lambda e: e.reciprocal(out=out, in_=in_), reads=[in_], writes=[out])

    def MEMSET(ap, v, eng="dve"):
        return P.op(eng, lambda e: e.memset(ap, v), writes=[ap])

    def MM(out, lhsT, rhs, start, stop):
        return P.op("pe", lambda e: e.matmul(out, lhsT=lhsT, rhs=rhs, start=start, stop=stop), reads=[lhsT, rhs], writes=[out])

    def TR(out, in_, ident):
        return P.op("pe", lambda e: e.transpose(out, in_, ident), reads=[in_, ident], writes=[out])

    def DMA(out, in_, eng=None, **kw):
        if eng is None:
            eng = "sync"
        return P.dma(eng, out, in_, **kw)

    def nxt(key, lst):
        cnt[key] = cnt.get(key, 0) + 1
        return lst[cnt[key] % len(lst)]

    DMA(idf[:], c_idf[:, :])
    DMA(idb[:], c_idb[:, :])
    DMA(rb33[0:32, :], relb[:, :])
    MEMSET(rb33[32:33, :], 1.0)

    def build_tables():
        AFl.reset()
        tb_in = [AFl.alloc(4, 512) for _ in range(2)]
        tb_out = [AFl.alloc(4, 512) for _ in range(2)]
        nchunk = NTAB // 2048
        rem = NTAB - nchunk * 2048
        pieces = [(i * 2048, 2048) for i in range(nchunk)]
        if rem:
            pieces.append((nchunk * 2048, rem))
        for pi, (c0, n) in enumerate(pieces):
            ti = tb_in[pi % 2]
            to = tb_out[pi % 2]
            tiv = ti.rearrange("p a b -> p (a b)")
            tov = to.rearrange("p a b -> p (a b)")
            DMA(tiv[0:33, :n], c_tab[:, c0:c0 + n])
            for j in range(n // 512):
                ps = psA[j % 4]
                MM(ps[0:16, :], lhsT=rb33[:, :], rhs=tiv[0:33, j * 512:(j + 1) * 512], start=True, stop=True)
                CP(tov[0:16, j * 512:(j + 1) * 512], ps[0:16, :])
            DMA(TB[:, c0:c0 + n], tov[0:16, :n])

    if upto != -2:
        build_tables()
    if upto == -1:
        return finish(nc, P, st)

    def build_tz():
        for name, L in TZL.items():
            off0, n = toffs[name]
            for h in range(16):
                src = bass.AP(tensor=TB.tensor, offset=h * NTAB + off0, ap=[[0, 128], [1, L]])
                DMA(TZ[name][h, :, :], src)

    if upto != -2:
        build_tz()

    def tz_ap(name, h, extra, fcount):
        L = TZL[name]
        return bass.AP(tensor=TZ[name].tensor, offset=h * 128 * L + extra, ap=[[L - 1, 128], [1, fcount]])

    def tb_ap(name, h, extra_off, pstep, pcount, fcount):
        off0, n = toffs[name]
        base = TB[h:h + 1, 0:1]
        return bass.AP(tensor=base.tensor, offset=base.offset + off0 + extra_off, ap=[[pstep, pcount], [1, fcount]])

    def norm_stage(Hsrc, gidx, aT, p_layer=None, pT=None):
        xts = [AFl.alloc(2048) for _ in range(2)]
        gb = AFl.alloc(2048)
        junk = AFl.alloc(2048)
        ats = [AB.alloc(2048) for _ in range(2)]
        pts = AFl.alloc(256) if pT is not None else None
        ptb = AB.alloc(256) if pT is not None else None
        DMA(gb, bass.AP(tensor=norms.tensor, offset=gidx * D, ap=[[0, 128], [1, D]]))
        for (i, t0, nt) in TILES:
            xt = xts[i % 2]
            at = ats[i % 2]
            ss = small[:, (i % 2) * 4:(i % 2) * 4 + 1]
            rs = small[:, (i % 2) * 4 + 1:(i % 2) * 4 + 2]
            DMA(xt[:nt], Hsrc[t0:t0 + nt, :])
            P.op("act", lambda e, xt=xt, ss=ss, nt=nt: e.activation(out=junk[:nt], in_=xt[:nt], func=AF.Square, accum_out=ss[:nt]),
                 reads=[xt[:nt]], writes=[junk[:nt], ss[:nt]])
            TS(rs[:nt], ss[:nt], 1.0 / D, ALU.mult, EPS, ALU.add)
            ACT(rs[:nt], rs[:nt], AF.Sqrt)
            RECIP(rs[:nt], rs[:nt])
            STT(at[:nt], xt[:nt], rs[:nt], gb[:nt], ALU.mult, ALU.mult)
            for q in range(4):
                pt = nxt("psT", psT)
                for j in range(4):
                    c = q * 4 + j
                    TR(pt[:, j * 128:j * 128 + nt], at[:nt, c * 128:(c + 1) * 128], idb[:nt, :nt])
                CP(aT[:, q * 4:(q + 1) * 4, t0:t0 + nt],
                   pt[:, 0:512].rearrange("p (j t) -> p j t", j=4)[:, :, :nt])
            if pT is not None:
                DMA(pts[:nt], pp[p_layer, t0:t0 + nt, :])
                CP(ptb[:nt], pts[:nt], eng="dve")
                pt = nxt("psT", psT)
                for j in range(2):
                    TR(pt[:, j * 128:j * 128 + nt], ptb[:nt, j * 128:(j + 1) * 128], idb[:nt, :nt])
                CP(pT[:, :, t0:t0 + nt], pt[:, 0:256].rearrange("p (j t) -> p j t", j=2)[:, :, :nt])

    def wload(w2d, k0, kch, n0, ncols, buf):
        src = w2d[k0 * 128:(k0 + kch) * 128, n0:n0 + ncols].rearrange("(kc p) n -> p kc n", p=128)
        for k in range(0, kch, 2):
            k2 = min(kch, k + 2)
            DMA(buf[:, k:k2, :ncols], src[:, k:k2, :], eng="pool")

    def mm_tok(ps, aT, t0, nt, wbuf, kch, ncols):
        for k in range(kch):
            MM(ps[:nt, :ncols], lhsT=aT[:, k, t0:t0 + nt], rhs=wbuf[:, k, :ncols], start=(k == 0), stop=(k == kch - 1))

    def mm_feat(ps, aT, t0, ntok, wbuf, kch, j):
        for k in range(kch):
            MM(ps[:, :ntok], lhsT=wbuf[:, k, j * 128:(j + 1) * 128], rhs=aT[:, k, t0:t0 + ntok], start=(k == 0), stop=(k == kch - 1))

    TOKBLK = [(0, 512), (512, 512), (1024, 512), (1536, 512), (2048, 4)]

    def l0_proj():
        AB.reset()
        AFl.reset()
        aT = AB.alloc(16, NT)
        wbs = [AB.alloc(16, 512) for _ in range(2)]
        norm_stage(x, 0, aT)
        if upto == -3:
            return
        fst = [AFl.alloc(512) for _ in range(3)]
        bst = [AB.alloc(512) for _ in range(3)]
        fq = [AB.alloc(NT) for _ in range(2)]
        blk = 0
        for g, (win, dil) in enumerate(DIL):
            Ls = S // dil
            keep = min(win, S)
            for part in range(3):
                for hb in range(4):
                    n0 = g * 6144 + part * 2048 + hb * 512
                    if blk >= int(os.environ.get('KLIMIT', '999')):
                        continue
                    wb = wbs[blk % 2]
                    blk += 1
                    wload(w_in_a, 0, 16, n0, 512, wb)
                    if part < 2:
                        dst = QT0 if part == 0 else KT0
                        for j in range(4):
                            h = hb * 4 + j
                            stg = nxt("fq", fq)
                            for (t0, ntok) in TOKBLK:
                                ps = nxt("psA", psA)
                                mm_feat(ps, aT, t0, ntok, wb, 16, j)
                                if ntok == 512:
                                    m0 = t0 // dil
                                    if dil == 1:
                                        CP(stg[:, t0:t0 + 512], ps[:, :512])
                                    else:
                                        ov = stg[:, 0:S].rearrange("p (r m) -> p m r", r=dil)[:, m0:m0 + 512 // dil, :]
                                        iv = ps[:, :512].rearrange("p (m r) -> p m r", r=dil)
                                        CP(ov, iv)
                                else:
                                    CP(stg[:, S:S + 4], ps[:, :4])
                            row = (g * 16 + h) * 128
                            DMA(dst[row:row + 128, :], stg[:, :])
                    if part >= 1:
                        kv = part - 1
                        for (i, t0, nt) in TILES:
                            in_keep = (i == 16) or (t0 >= S - keep)
                            if part == 1 and not in_keep:
                                continue
                            if part == 2 and i < int(os.environ.get('KTMIN', '0')):
                                continue
                            ps = nxt("psA", psA)
                            mm_tok(ps, aT, t0, nt, wb, 16, 512)
                            sf = None
                            if in_keep:
                                sf = nxt("fst", fst)
                                CP(sf[:nt], ps[:nt, :])
                                if i == 16:
                                    DMA(o_ds[g][win - 4:win, kv, hb * 512:(hb + 1) * 512], sf[:nt])
                                else:
                                    r0 = t0 - (S - keep)
                                    DMA(o_dp[g][r0:r0 + nt, kv, hb * 512:(hb + 1) * 512], sf[:nt])
                            if part == 2:
                                sb = nxt("bst", bst)
                                CP(sb[:nt], sf[:nt] if sf is not None else ps[:nt, :])
                                DMA(V0[g, t0:t0 + nt, hb * 512:(hb + 1) * 512], sb[:nt])
        if upto == -4:
            return
        for g, (win, dil) in enumerate(DIL):
            stt_ = (st128, st512, st2048)[g]
            nrow = win - 4
            step = 512
            for r0 in range(0, nrow, step):
                n = min(step, nrow - r0)
                DMA(o_ds[g][r0:r0 + n, :, :], stt_[4 + r0:4 + r0 + n, :, :])

    l0_proj()
    if upto <= 1:
        return finish(nc, P, st)

    def softmax_pv(nq, ps_chunks, nk, bias_ap, vget, o_dst, nm_dst, l_dst, ss, pb, pts):
        for (pa, c0, n) in ps_chunks:
            STT(ss[:nq, c0:c0 + n], pa, SCALE, bias_ap[:, c0:c0 + n], ALU.mult, ALU.add)
        RED(nm_dst, ss[:nq, :nk], ALU.max, negate=True)
        ACT(pb[:nq, :nk], ss[:nq, :nk], AF.Exp, bias=nm_dst, scale=1.0)
        RED(l_dst, pb[:nq, :nk], ALU.add)
        cnt["rl"] = cnt.get("rl", 0) + 1
        rl = small[:nq, 64 + cnt["rl"] % 64:65 + cnt["rl"] % 64]
        RECIP(rl, l_dst)
        nch = (nk + 127) // 128
        cnt["pv"] = cnt.get("pv", 0) + 1
        ceng = "act" if cnt["pv"] % 2 else "dve"
        per = max(1, 1024 // nq)
        for j0 in range(0, nch, per):
            pt = nxt("psT", psT)
            for j in range(j0, min(nch, j0 + per)):
                kc = min(128, nk - j * 128)
                TR(pt[:kc, (j - j0) * nq:(j - j0 + 1) * nq], pb[:nq, j * 128:j * 128 + kc], idb[:nq, :nq])
            for j in range(j0, min(nch, j0 + per)):
                kc = min(128, nk - j * 128)
                CP(pts[:kc, j, :nq], pt[:kc, (j - j0) * nq:(j - j0 + 1) * nq], eng=ceng)
        po = nxt("psO", psO)
        for j in range(nch):
            va, kc = vget(j)
            MM(po[:nq, :128], lhsT=pts[:kc, j, :nq], rhs=va, start=(j == 0), stop=(j == nch - 1))
        ACT(o_dst, po[:nq, :128], AF.Copy, scale=rl)

    def l0_attn():
        AB.reset()
        AFl.reset()
        QTb = [AB.alloc(16, 128) for _ in range(2)]
        KTb = [AB.alloc(16, 256) for _ in range(2)]
        Vb = [AB.alloc(2, 2048) for _ in range(2)]
        Pb = [AB.alloc(640) for _ in range(2)]
        PTs = [AB.alloc(5, 128) for _ in range(2)]
        bias_g = AFl.alloc(16, 256)
        Ssb = [AFl.alloc(640) for _ in range(2)]
        Ot = [AFl.alloc(2048) for _ in range(2)]
        mlt = [AFl.alloc(32) for _ in range(2)]
        for g, (win, dil) in enumerate(DIL):
            Ls = S // dil
            nb = Ls // 128
            for h in range(16):
                DMA(bias_g[:, h, :], tz_ap("l0p%d" % g, h, 127, 256))
            for r in range(dil):
                for b in range(nb):
                    col0 = r * Ls + b * 128
                    nprev = 1 if b > 0 else 0
                    nk = 128 * (1 + nprev)
                    kcol0 = col0 - 128 * nprev
                    qt = nxt("QTb", QTb)
                    kt = nxt("KTb", KTb)
                    vb = nxt("Vb", Vb)
                    for h4 in range(4):
                        rows = slice(g * 2048 + h4 * 512, g * 2048 + (h4 + 1) * 512)
                        DMA(qt[:, h4 * 4:(h4 + 1) * 4, :], QT0[rows, col0:col0 + 128].rearrange("(h d) c -> d h c", d=128))
                        DMA(kt[:, h4 * 4:(h4 + 1) * 4, :nk], KT0[rows, kcol0:kcol0 + nk].rearrange("(h d) c -> d h c", d=128))
                    for j in range(1 + nprev):
                        bb = b - nprev + j
                        row0 = bb * 128 * dil + r
                        DMA(vb[:, j, :], bass.AP(tensor=V0.tensor, offset=g * NT * D + row0 * D, ap=[[dil * D, 128], [1, D]]))
                    ot = nxt("Ot", Ot)
                    ml = nxt("mlt", mlt)
                    for h in range(16):
                        ps = nxt("psA", psA)
                        MM(ps[:, :nk], lhsT=qt[:, h, :], rhs=kt[:, h, :nk], start=True, stop=True)
                        softmax_pv(128, [(ps[:, :nk], 0, nk)], nk, bias_g[:, h, 256 - nk:256],
                                   lambda j, vb=vb, h=h: (vb[:, j, h * 128:(h + 1) * 128], 128),
                                   ot[:, h * 128:(h + 1) * 128], ml[:, h:h + 1], ml[:, 16 + h:17 + h],
                                   nxt("Ssb", Ssb), nxt("Pb", Pb), nxt("PTs", PTs))
                    rq0 = b * 128 * dil + r
                    DMA(bass.AP(tensor=OG.tensor, offset=g * NT * D + rq0 * D, ap=[[dil * D, 128], [1, D]]), ot)
                    DMA(bass.AP(tensor=ML.tensor, offset=g * NT * 32 + rq0 * 32, ap=[[dil * 32, 128], [1, 32]]), ml)

    def l0_attn_sample():
        AB.reset()
        AFl.reset()
        KTs = AB.alloc(16, 516)
        Vs = AB.alloc(5, 2048)
        qts = AB.alloc(16, 4)
        Pb = [AB.alloc(640) for _ in range(2)]
        PTs = [AB.alloc(5, 128) for _ in range(2)]
        kst = [AFl.alloc(2048) for _ in range(2)]
        bias_sb = [AFl.alloc(516) for _ in range(2)]
        Ssb = [AFl.alloc(640) for _ in range(2)]
        ot = AFl.alloc(2048)
        ml = AFl.alloc(32)
        for g, (win, dil) in enumerate(DIL):
            stt_ = (st128, st512, st2048)[g]
            ntile = 1 if g == 0 else 4
            nkey = ntile * 128
            nk = nkey + 4
            for ti in range(ntile):
                if g < 2:
                    off_r, pstep = ti * 128, 1
                else:
                    off_r, pstep = ti, 16
                for kv in range(2):
                    kb = nxt("kst", kst)
                    DMA(kb, bass.AP(tensor=stt_.tensor, offset=off_r * 2 * D + kv * D, ap=[[pstep * 2 * D, 128], [1, D]]))
                    if kv == 0:
                        for h in range(16):
                            pq = nxt("psO", psO)
                            TR(pq[:, :128], kb[:, h * 128:(h + 1) * 128], idf[:, :])
                            CP(KTs[:, h, ti * 128:(ti + 1) * 128], pq[:, :128])
                    else:
                        CP(Vs[:, ti, :], kb)
            DMA(KTs[:, :, nkey:nkey + 4], KT0[g * 2048:(g + 1) * 2048, S:S + 4].rearrange("(h d) c -> d h c", d=128))
            DMA(qts[:, :, :], QT0[g * 2048:(g + 1) * 2048, S:S + 4].rearrange("(h d) c -> d h c", d=128))
            DMA(Vs[0:4, ntile, :], V0[g, S:S + 4, :])
            for h in range(16):
                bias_s = nxt("bias_sb", bias_sb)
                DMA(bias_s[0:4, :nk], tb_ap("l0s%d" % g, h, 0, nk, 4, nk))
                chunks = []
                c0 = 0
                while c0 < nk:
                    n = min(512, nk - c0)
                    ps = nxt("psA", psA)
                    MM(ps[:4, :n], lhsT=qts[:, h, :], rhs=KTs[:, h, c0:c0 + n], start=True, stop=True)
                    chunks.append((ps[:4, :n], c0, n))
                    c0 += n
                softmax_pv(4, chunks, nk, bias_s[0:4, :nk],
                           lambda j, h=h, nk=nk: (Vs[:min(128, nk - j * 128), j, h * 128:(h + 1) * 128], min(128, nk - j * 128)),
                           ot[0:4, h * 128:(h + 1) * 128], ml[0:4, h:h + 1], ml[0:4, 16 + h:17 + h],
                           nxt("Ssb", Ssb), nxt("Pb", Pb), nxt("PTs", PTs))
            DMA(OG[g, S:S + 4, :], ot[0:4, :])
            DMA(ML[g, S:S + 4, :], ml[0:4, :])

    def out_proj(aT, w2d, Hsrc, Hdst, kch=16, lhs_loader=None):
        wbs = [AB.alloc(kch if kch <= 16 else 22, 512) for _ in range(2)]
        hr = [AFl.alloc(512) for _ in range(3)]
        for nb_ in range(4):
            wb = wbs[nb_ % 2]
            wload(w2d, 0, kch, nb_ * 512, 512, wb)
            for (i, t0, nt) in TILES:
                ps = nxt("psA", psA)
                mm_tok(ps, aT, t0, nt, wb, kch, 512)
                h_ = nxt("hr", hr)
                DMA(h_[:nt], Hsrc[t0:t0 + nt, nb_ * 512:(nb_ + 1) * 512])
                TT(h_[:nt], ps[:nt, :], h_[:nt], ALU.add)
                DMA(Hdst[t0:t0 + nt, nb_ * 512:(nb_ + 1) * 512], h_[:nt])

    def l0_combine_out():
        AB.reset()
        AFl.reset()
        aT = AB.alloc(16, NT)
        ogs = [AFl.alloc(2048) for _ in range(3)]
        acc = AFl.alloc(2048)
        mls = [AFl.alloc(32) for _ in range(3)]
        wk = AFl.alloc(8, 16)
        at = AB.alloc(2048)
        for (i, t0, nt) in TILES:
            for g in range(3):
                DMA(ogs[g][:nt], OG[g, t0:t0 + nt, :])
                DMA(mls[g][:nt], ML[g, t0:t0 + nt, :])
            mn = wk[:nt, 0, :]
            TT(mn, mls[0][:nt, 0:16], mls[1][:nt, 0:16], ALU.min)
            TT(mn, mn, mls[2][:nt, 0:16], ALU.min)
            den = wk[:nt, 1, :]
            for g in range(3):
                e = wk[:nt, 2 + g, :]
                TT(e, mn, mls[g][:nt, 0:16], ALU.subtract)
                ACT(e, e, AF.Exp)
                TT(e, e, mls[g][:nt, 16:32], ALU.mult)
                if g == 0:
                    CP(den, e, eng="dve")
                else:
                    TT(den, den, e, ALU.add)
            RECIP(den, den)
            for g in range(3):
                e = wk[:nt, 2 + g, :]
                TT(e, e, den, ALU.mult)
                ov = ogs[g][:nt].rearrange("p (h d) -> p h d", h=16)
                wv = e.unsqueeze(2).to_broadcast([nt, 16, 128])
                if g == 0:
                    TT(acc[:nt].rearrange("p (h d) -> p h d", h=16), ov, wv, ALU.mult)
                else:
                    TT(ov, ov, wv, ALU.mult)
                    TT(acc[:nt], acc[:nt], ogs[g][:nt], ALU.add)
            CP(at[:nt], acc[:nt], eng="act")
            for q in range(4):
                pt = nxt("psT", psT)
                for j in range(4):
                    c = q * 4 + j
                    TR(pt[:, j * 128:j * 128 + nt], at[:nt, c * 128:(c + 1) * 128], idb[:nt, :nt])
                CP(aT[:, q * 4:(q + 1) * 4, t0:t0 + nt],
                   pt[:, 0:512].rearrange("p (j t) -> p j t", j=4)[:, :, :nt])
        out_proj(aT, w_out_a, x, HA)

    l0_attn()
    l0_attn_sample()
    if upto <= 2:
        return finish(nc, P, st)
    l0_combine_out()
    if upto <= 3:
        return finish(nc, P, st)

    def load_cols_T(dst, src2d, nrow, tmp):
        for r in range(nrow):
            DMA(tmp[0:44, :], src2d[r, :].rearrange("(c p) -> c p", p=128))
            pq = nxt("psO", psO)
            TR(pq[:, :44], tmp[0:44, :], idf[0:44, 0:44])
            CP(dst[:, :, r], pq[:, :44])

    def ffn(li, Hsrc, Hdst):
        AB.reset()
        AFl.reset()
        aT = AB.alloc(16, NT)
        wbs = [AB.alloc(16, 512) for _ in range(2)]
        norm_stage(Hsrc, 1 + 3 * li, aT)
        AFl.reset()
        Gs = [AFl.alloc(2056) for _ in range(2)]
        Cs = [AFl.alloc(2056) for _ in range(2)]
        cwb = AFl.alloc(44, 4)
        tmp = AFl.alloc(128)
        uTst = [AB.alloc(NT) for _ in range(2)]
        load_cols_T(cwb[:, :, 0:3], conv_w[li], 3, tmp)
        load_cols_T(cwb[:, :, 3:4], conv_b[li:li + 1, :], 1, tmp)
        for G in Gs:
            MEMSET(G[:, 0:2], 0.0)
        for fb in range(11):
            wg, wv = wbs
            wload(w_ffn_in[li], 0, 16, fb * 512, 512, wg)
            wload(w_ffn_in[li], 0, 16, DFF + fb * 512, 512, wv)
            for j in range(4):
                c = fb * 4 + j
                G = nxt("Gs", Gs)
                C = nxt("Cs", Cs)
                for (t0, ntok) in TOKBLK:
                    ps = nxt("psA", psA)
                    mm_feat(ps, aT, t0, ntok, wg, 16, j)
                    if ntok == 512:
                        CP(G[:, 2 + t0:2 + t0 + 512], ps[:, :512])
                    else:
                        CP(G[:, 2052:2056], ps[:, :4])
                DMA(G[:, 2050:2052], stconv[li, :, c * 128:(c + 1) * 128].rearrange("t p -> p t"), allow_slow_non_contiguous=True)
                w0, w1, w2, bb = (cwb[:, c, k:k + 1] for k in range(4))
                for (o0, n, g0) in ((0, 2048, 0), (2048, 4, 2050)):
                    ACT(C[:, o0:o0 + n], G[:, g0 + 2:g0 + 2 + n], AF.Identity, bias=bb, scale=w2)
                    STT(C[:, o0:o0 + n], G[:, g0 + 1:g0 + 1 + n], w1, C[:, o0:o0 + n], ALU.mult, ALU.add)
                    STT(C[:, o0:o0 + n], G[:, g0:g0 + n], w0, C[:, o0:o0 + n], ALU.mult, ALU.add)
                ACT(C[:, :NT], C[:, :NT], AF.Gelu_apprx_tanh)
                DMA(o_convp[li, :, c * 128:(c + 1) * 128].rearrange("t p -> p t"), G[:, 2048:2050], allow_slow_non_contiguous=True)
                DMA(o_convs[li, :, c * 128:(c + 1) * 128].rearrange("t p -> p t"), G[:, 2054:2056], allow_slow_non_contiguous=True)
                us = nxt("uTst", uTst)
                for (t0, ntok) in TOKBLK:
                    ps = nxt("psA", psA)
                    mm_feat(ps, aT, t0, ntok, wv, 16, j)
                    TT(us[:, t0:t0 + ntok], ps[:, :ntok], C[:, t0:t0 + ntok], ALU.mult)
                DMA(UT[c * 128:(c + 1) * 128, :], us)
        AB.reset()
        AFl.reset()
        wbs2 = [AB.alloc(44, 512) for _ in range(2)]
        uts = [AB.alloc(44, 128) for _ in range(2)]
        hr = [AFl.alloc(512) for _ in range(3)]
        for nb_ in range(4):
            wb = wbs2[nb_ % 2]
            wload(w_ffn_out[li], 0, 44, nb_ * 512, 512, wb)
            for (i, t0, nt) in TILES:
                ut = nxt("uts", uts)
                for q in range(4):
                    DMA(ut[:, q * 11:(q + 1) * 11, :nt],
                        UT[q * 11 * 128:(q + 1) * 11 * 128, t0:t0 + nt].rearrange("(c p) t -> p c t", p=128))
                ps = nxt("psA", psA)
                for k in range(44):
                    MM(ps[:nt, :512], lhsT=ut[:, k, :nt], rhs=wb[:, k, :512], start=(k == 0), stop=(k == 43))
                h_ = nxt("hr", hr)
                DMA(h_[:nt], Hsrc[t0:t0 + nt, nb_ * 512:(nb_ + 1) * 512])
                TT(h_[:nt], ps[:nt, :], h_[:nt], ALU.add)
                DMA(Hdst[t0:t0 + nt, nb_ * 512:(nb_ + 1) * 512], h_[:nt])

    def ple(li, Hsrc, Hdst):
        AB.reset()
        AFl.reset()
        aT = AB.alloc(16, NT)
        pT = AB.alloc(2, NT)
        wbs = [AB.alloc(16, 512) for _ in range(2)]
        wps = [AB.alloc(2, 512) for _ in range(2)]
        norm_stage(Hsrc, 2 + 3 * li, aT, p_layer=li, pT=pT)
        AFl.reset()
        hr = [AFl.alloc(512) for _ in range(3)]
        gs = [AFl.alloc(512) for _ in range(3)]
        for nb_ in range(4):
            wb = wbs[nb_ % 2]
            wp = wps[nb_ % 2]
            wload(w_ple_gate[li], 0, 16, nb_ * 512, 512, wb)
            wload(w_ple_proj[li], 0, 2, nb_ * 512, 512, wp)
            for (i, t0, nt) in TILES:
                ps = nxt("psA", psA)
                mm_tok(ps, aT, t0, nt, wb, 16, 512)
                g_ = nxt("gs", gs)
                ACT(g_[:nt], ps[:nt, :], AF.Sigmoid)
                if not os.environ.get("KPLE"):
                    ps2 = nxt("psA", psA)
                    mm_tok(ps2, pT, t0, nt, wp, 2, 512)
                    TT(g_[:nt], ps2[:nt, :], g_[:nt], ALU.mult)
                h_ = nxt("hr", hr)
                DMA(h_[:nt], Hsrc[t0:t0 + nt, nb_ * 512:(nb_ + 1) * 512])
                TT(h_[:nt], h_[:nt], g_[:nt], ALU.add)
                DMA(Hdst[t0:t0 + nt, nb_ * 512:(nb_ + 1) * 512], h_[:nt])

    ffn(0, HA, HB)
    if upto <= 4:
        return finish(nc, P, st)
    ple(0, HB, HA)
    if upto <= 5:
        return finish(nc, P, st)

    def l1_proj():
        AB.reset()
        AFl.reset()
        aT = AB.alloc(16, NT)
        wbs = [AB.alloc(16, 512) for _ in range(2)]
        norm_stage(HA, 3, aT)
        AFl.reset()
        fst = [AFl.alloc(512) for _ in range(3)]
        bst = [AB.alloc(512) for _ in range(3)]
        fq = [AB.alloc(NT) for _ in range(2)]
        for blk in range(11):
            wb = wbs[blk % 2]
            ncols = 512 if blk < 10 else 48
            wload(w_in_b, 0, 16, blk * 512, ncols, wb)
            feat_dst = None
            if blk < 4:
                feat_dst = (QT1, blk * 512)
            elif blk in (4, 5):
                feat_dst = (XT1, (blk - 4) * 512)
            elif blk == 6:
                feat_dst = (KST, 0)
            elif blk == 8:
                feat_dst = (KWT, 0)
            if feat_dst is not None:
                dst, r0 = feat_dst
                for j in range(4):
                    stg = nxt("fq", fq)
                    for (t0, ntok) in TOKBLK:
                        ps = nxt("psA", psA)
                        mm_feat(ps, aT, t0, ntok, wb, 16, j)
                        CP(stg[:, t0:t0 + ntok], ps[:, :ntok])
                    DMA(dst[r0 + j * 128:r0 + (j + 1) * 128, :], stg[:, :])
            if blk >= 4:
                for (i, t0, nt) in TILES:
                    need_f32 = (4 <= blk <= 7) or (blk in (8, 9) and (i == 16 or t0 >= S - 512)) or blk == 10
                    need_b16 = blk in (7, 9)
                    if not (need_f32 or need_b16):
                        continue
                    ps = nxt("psA", psA)
                    mm_tok(ps, aT, t0, nt, wb, 16, ncols)
                    sf = None
                    if need_f32:
                        sf = nxt("fst", fst)
                        if blk == 10:
                            ACT(sf[:nt, :48], ps[:nt, :48], AF.Sigmoid)
                            DMA(G1[t0:t0 + nt, :], sf[:nt, :48])
                        else:
                            CP(sf[:nt], ps[:nt, :])
                            if blk <= 7:
                                kvi = blk - 4
                                if i == 16:
                                    DMA(o_kvs[:, kvi * 512:(kvi + 1) * 512], sf[:nt])
                                else:
                                    DMA(o_kvp[t0:t0 + nt, kvi * 512:(kvi + 1) * 512], sf[:nt])
                            else:
                                kv = blk - 8
                                if i == 16:
                                    DMA(o_wins[508:512, kv, :], sf[:nt])
                                else:
                                    DMA(o_winp[t0 - (S - 512):t0 - (S - 512) + nt, kv, :], sf[:nt])
                    if need_b16:
                        sb = nxt("bst", bst)
                        CP(sb[:nt], sf[:nt] if sf is not None else ps[:nt, :])
                        DMA((VS1 if blk == 7 else VW1)[t0:t0 + nt, :], sb[:nt])
        DMA(o_wins[0:508, :, :], stwin[4:512, :, :])

    l1_proj()
    if upto <= 6:
        return finish(nc, P, st)

    def sub_ap(v, extra, fstep, fcount):
        return bass.AP(tensor=v.tensor, offset=v.offset + extra, ap=[list(v.ap[0]), [fstep, fcount]])

    def compress_setup(kvi, w1b, w2b, peTb, cols, tmpf):
        for r0 in range(0, 32, 4):
            DMA(w1b[:, r0:r0 + 4, :], cmp_w1[kvi, r0:r0 + 4, :, :].rearrange("r d f -> d r f"), eng="pool")
        DMA(w2b[:, :], cmp_w2[kvi, :, :], eng="pool")
        DMA(tmpf[0:32, :], cmp_pe[kvi, :, :])
        pq = nxt("psO", psO)
        TR(pq[:, :32], tmpf[0:32, :], idf[0:32, 0:32])
        CP(peTb[:, :], pq[:, :32])
        DMA(cols[:, 2:3], cmp_b1[kvi:kvi + 1, :].rearrange("o f -> f o"), allow_slow_non_contiguous=True)
        DMA(cols[:, 1:2], cmp_b2[kvi:kvi + 1, :].rearrange("o f -> f o"), allow_slow_non_contiguous=True)
        pq = nxt("psO", psO)
        for r in range(32):
            MM(pq[:, 0:1], lhsT=w1b[:, r, :], rhs=peTb[:, r:r + 1], start=(r == 0), stop=(r == 31))
        TT(cols[:, 0:1], pq[:, 0:1], cols[:, 2:3], ALU.add)

    def nsa_prompt():
        AB.reset()
        AFl.reset()
        KsT = AB.alloc(4, S)
        KwT = AB.alloc(4, S)
        Vsel = AB.alloc(16, 512)
        Vwin = AB.alloc(16, 512)
        kcT = AB.alloc(4, 128)
        vc = AB.alloc(4, 128)
        w1b = AB.alloc(32, 128)
        w2b = AB.alloc(128)
        peTb = AB.alloc(32)
        XTt = AB.alloc(S)
        hg = AB.alloc(128)
        QTq = [AB.alloc(16, 128) for _ in range(2)]
        Pb = [AB.alloc(2048) for _ in range(2)]
        PTs = [AB.alloc(16, 128) for _ in range(2)]
        obf = AB.alloc(2048)
        Ssb = [AFl.alloc(2048) for _ in range(2)]
        biasb = [AFl.alloc(2048) for _ in range(2)]
        oacc = AFl.alloc(2048)
        pcp = AFl.alloc(4, 132)
        cols = AFl.alloc(8)
        tmpf = AFl.alloc(128)
        b2row = AFl.alloc(128)
        gts = AFl.alloc(48)
        sA = AFl.alloc(32)
        sB = AFl.alloc(32)
        wk = AFl.alloc(8, 32)
        tmpO = AFl.alloc(128)
        for kvh in range(4):
            DMA(KsT[:, kvh, :], KST[kvh * 128:(kvh + 1) * 128, 0:S])
            DMA(KwT[:, kvh, :], KWT[kvh * 128:(kvh + 1) * 128, 0:S])
        for q in range(4):
            DMA(Vsel[:, q * 4:(q + 1) * 4, :], VS1[q * 512:(q + 1) * 512, :].rearrange("(b p) c -> p b c", p=128))
            DMA(Vwin[:, q * 4:(q + 1) * 4, :], VW1[q * 512:(q + 1) * 512, :].rearrange("(b p) c -> p b c", p=128))
        for kvi in range(2):
            compress_setup(kvi, w1b, w2b, peTb, cols, tmpf)
            if kvi == 1:
                DMA(b2row[:, :], bass.AP(tensor=cmp_b2.tensor, offset=128, ap=[[0, 128], [1, 128]]))
            for kvh in range(4):
                DMA(XTt[:, :], XT1[(kvi * 4 + kvh) * 128:(kvi * 4 + kvh + 1) * 128, 0:S])
                ps = nxt("psA", psA)
                for r in range(32):
                    MM(ps[:, :127], lhsT=w1b[:, r, :], rhs=sub_ap(XTt, r, 16, 127), start=(r == 0), stop=(r == 31))
                ACT(hg[:, :127], ps[:, :127], AF.Gelu_apprx_tanh, bias=cols[:, 0:1], scale=1.0)
                pq = nxt("psO", psO)
                if kvi == 0:
                    MM(pq[:, :127], lhsT=w2b[:, :], rhs=hg[:, :127], start=True, stop=True)
                    ACT(kcT[:, kvh, :127], pq[:, :127], AF.Identity, bias=cols[:, 1:2], scale=1.0)
                else:
                    MM(pq[:127, :128], lhsT=hg[:, :127], rhs=w2b[:, :], start=True, stop=True)
                    TT(vc[:127, kvh, :], pq[:127, :128], b2row[:127, :], ALU.add)
        MEMSET(pcp[:, :, :], 0.0)
        for b in range(16):
            t0 = b * 128
            qt = nxt("QTq", QTq)
            for h4 in range(4):
                DMA(qt[:, h4 * 4:(h4 + 1) * 4, :], QT1[h4 * 512:(h4 + 1) * 512, t0:t0 + 128].rearrange("(h d) c -> d h c", d=128))
            DMA(gts[:, :], G1[t0:t0 + 128, :])
            DMA(sA[:, :], c_selA[t0:t0 + 128, :])
            DMA(sB[:, :], c_selB[t0:t0 + 128, :])
            DMA(cols[:, 4:5], c_rowv[t0:t0 + 128, :])
            nk_s = 128 * (b + 1)
            kb0 = max(0, b - 4)
            nk_w = 128 * (b - kb0 + 1)
            for kvh in range(4):
                for g in range(4):
                    h = kvh * 4 + g
                    ps = nxt("psA", psA)
                    MM(ps[:, :127], lhsT=qt[:, h, :], rhs=kcT[:, kvh, :127], start=True, stop=True)
                    bb_ = nxt("biasb", biasb)
                    DMA(bb_[:, :127], tb_ap("cmpp", h, 120 - 8 * b, 247, 128, 127))
                    ss = nxt("Ssb", Ssb)
                    STT(ss[:, :127], ps[:, :127], SCALE, bb_[:, :127], ALU.mult, ALU.add)
                    nm = wk[:, 0, 0:1]
                    l_ = wk[:, 0, 1:2]
                    rl = wk[:, 0, 2:3]
                    RED(nm, ss[:, :127], ALU.max, negate=True)
                    ACT(ss[:, 128:255], ss[:, :127], AF.Exp, bias=nm, scale=1.0)
                    RED(l_, ss[:, 128:255], ALU.add)
                    TS(l_, l_, 1e-30, ALU.max)
                    RECIP(rl, l_)
                    TT(rl, rl, cols[:, 4:5], ALU.mult)
                    TS(ss[:, 128:255], ss[:, 128:255], rl, ALU.mult)
                    if g == 0:
                        CP(pcp[:, kvh, 1:128], ss[:, 128:255], eng="dve")
                    else:
                        TT(pcp[:, kvh, 1:128], pcp[:, kvh, 1:128], ss[:, 128:255], ALU.add)
                    pb = nxt("Pb", Pb)
                    CP(pb[:, :127], ss[:, 128:255], eng="act")
                    pt = nxt("psT", psT)
                    TR(pt[:127, :128], pb[:, :127], idb[:, :])
                    pts = nxt("PTs", PTs)
                    CP(pts[:127, 0, :], pt[:127, :128])
                    po = nxt("psO", psO)
                    MM(po[:, :128], lhsT=pts[:127, 0, :], rhs=vc[:127, kvh, :], start=True, stop=True)
                    ACT(oacc[:, h * 128:(h + 1) * 128], po[:, :128], AF.Copy, scale=gts[:, h:h + 1])
                pv = pcp[:, kvh, :]
                va = pv[:, 0:128].rearrange("p (j f) -> p j f", f=4)
                vb_ = pv[:, 4:132].rearrange("p (j f) -> p j f", f=4)
                t1 = wk[:, 1, :]
                psl = wk[:, 2, :]
                sc = wk[:, 3, :]
                sc2 = wk[:, 4, :]
                m8 = wk[:, 5, 0:8]
                m8b = wk[:, 5, 8:16]
                mneg = wk[:, 6, :]
                TT(t1, va[:, :, 1], va[:, :, 2], ALU.add)
                TT(t1, t1, va[:, :, 3], ALU.add)
                STT(psl, t1, 2.0, va[:, :, 0], ALU.mult, ALU.add)
                TT(psl, psl, vb_[:, :, 0], ALU.add)
                TT(sc, psl, sA[:, :], ALU.mult)
                TT(sc, sc, sB[:, :], ALU.add)
                P.op("dve", lambda e, m8=m8, sc=sc: e.max(out=m8, in_=sc), reads=[sc], writes=[m8])
                P.op("dve", lambda e, m8=m8, sc=sc, sc2=sc2: e.match_replace(out=sc2, in_to_replace=m8, in_values=sc, imm_value=-1e9),
                     reads=[sc, m8], writes=[sc2])
                P.op("dve", lambda e, m8b=m8b, sc2=sc2: e.max(out=m8b, in_=sc2), reads=[sc2], writes=[m8b])
                TS(mneg, sc, m8b[:, 7:8], ALU.is_ge)
                TS(mneg, mneg, -1.0, ALU.add, -NEG, ALU.mult)
                for g in range(4):
                    h = kvh * 4 + g
                    for br in (1, 2):
                        if br == 1:
                            nk, kc0, KT_, V_, vb0 = nk_s, 0, KsT, Vsel, 0
                        else:
                            nk, kc0, KT_, V_, vb0 = nk_w, kb0 * 128, KwT, Vwin, kb0
                        chunks = []
                        c0 = 0
                        while c0 < nk:
                            n = min(512, nk - c0)
                            ps = nxt("psA", psA)
                            MM(ps[:, :n], lhsT=qt[:, h, :], rhs=KT_[:, kvh, kc0 + c0:kc0 + c0 + n], start=True, stop=True)
                            chunks.append((ps[:, :n], c0, n))
                            c0 += n
                        bb_ = nxt("biasb", biasb)
                        if br == 1:
                            DMA(bb_[:, :nk], tz_ap("selp", h, 127 + 1920 - 128 * b, nk))
                            TT(bb_[:, :nk].rearrange("p (j s) -> p j s", s=64), bb_[:, :nk].rearrange("p (j s) -> p j s", s=64),
                               mneg[:, 0:nk // 64].unsqueeze(2).to_broadcast([128, nk // 64, 64]), ALU.add)
                        else:
                            DMA(bb_[:, :nk], tz_ap("winp", h, 127 + 640 - nk, nk))
                        softmax_pv(128, chunks, nk, bb_[:, :nk],
                                   lambda j, V_=V_, vb0=vb0, kvh=kvh: (V_[:, vb0 + j, kvh * 128:(kvh + 1) * 128], 128),
                                   tmpO[:, :], wk[:, 7, 0:1], wk[:, 7, 1:2],
                                   nxt("Ssb", Ssb), nxt("Pb", Pb), nxt("PTs", PTs))
                        STT(oacc[:, h * 128:(h + 1) * 128], tmpO[:, :], gts[:, br * 16 + h:br * 16 + h + 1],
                            oacc[:, h * 128:(h + 1) * 128], ALU.mult, ALU.add)
            CP(obf[:, :], oacc[:, :], eng="act")
            DMA(OM[t0:t0 + 128, :], obf[:, :])

    def mixer_out(w2d, Hsrc, Hdst):
        AB.reset()
        AFl.reset()
        aT = AB.alloc(16, NT)
        ats = [AB.alloc(2048) for _ in range(2)]
        for (i, t0, nt) in TILES:
            at = ats[i % 2]
            DMA(at[:nt], OM[t0:t0 + nt, :])
            for q in range(4):
                pt = nxt("psT", psT)
                for j in range(4):
                    c = q * 4 + j
                    TR(pt[:, j * 128:j * 128 + nt], at[:nt, c * 128:(c + 1) * 128], idb[:nt, :nt])
                CP(aT[:, q * 4:(q + 1) * 4, t0:t0 + nt],
                   pt[:, 0:512].rearrange("p (j t) -> p j t", j=4)[:, :, :nt])
        out_proj(aT, w2d, Hsrc, Hdst)

    def final_norm(Hsrc):
        AB.reset()
        AFl.reset()
        xts = [AFl.alloc(2048) for _ in range(2)]
        gb = AFl.alloc(2048)
        junk = AFl.alloc(2048)
        ys = [AFl.alloc(2048) for _ in range(2)]
        DMA(gb, bass.AP(tensor=norms.tensor, offset=6 * D, ap=[[0, 128], [1, D]]))
        for (i, t0, nt) in TILES:
            xt = xts[i % 2]
            yt = ys[i % 2]
            ss = small[:, 16 + (i % 2) * 4:16 + (i % 2) * 4 + 1]
            rs = small[:, 16 + (i % 2) * 4 + 1:16 + (i % 2) * 4 + 2]
            DMA(xt[:nt], Hsrc[t0:t0 + nt, :])
            P.op("act", lambda e, xt=xt, ss=ss, nt=nt: e.activation(out=junk[:nt], in_=xt[:nt], func=AF.Square, accum_out=ss[:nt]),
                 reads=[xt[:nt]], writes=[junk[:nt], ss[:nt]])
            TS(rs[:nt], ss[:nt], 1.0 / D, ALU.mult, EPS, ALU.add)
            ACT(rs[:nt], rs[:nt], AF.Sqrt)
            RECIP(rs[:nt], rs[:nt])
            STT(yt[:nt], xt[:nt], rs[:nt], gb[:nt], ALU.mult, ALU.mult)
            DMA(o_y[t0:t0 + nt, :], yt[:nt])

    def nsa_sample():
        AB.reset()
        AFl.reset()
        pgs = [AFl.alloc(2048) for _ in range(2)]
        ptf = AFl.alloc(128)
        stg = [AB.alloc(12, 512) for _ in range(2)]
        vst = [AB.alloc(512) for _ in range(2)]
        DMA(pt_i32[:, :], bass.AP(tensor=ptab.tensor, offset=0, ap=[[0, 128], [1, 128]]))
        DMA(iota_sb[:, :], c_iota[:, :])
        DMA(selm_sb[:, :], c_selm[:, :])
        DMA(selmT_sb[:, :], c_selm.rearrange("a b -> b a"), allow_slow_non_contiguous=True)
        CP(ptf[:, :], pt_i32[:, :], eng="dve")
        TS(ptf[:, :], ptf[:, :], 128.0, ALU.mult, iota_sb[:, 0:1], ALU.add)
        CP(idx_i32[:, :], ptf[:, :], eng="dve")
        for n in range(NPAGE):
            pg = pgs[n % 2]
            sg = stg[(n // 4) % 2]
            P.dma_custom("pool", lambda e, pg=pg, n=n: e.indirect_dma_start(
                out=pg[:, :], out_offset=None, in_=cache[:, :],
                in_offset=bass.IndirectOffsetOnAxis(ap=idx_i32[:, n:n + 1], axis=0)),
                reads=[cache[:, :], idx_i32[:, n:n + 1]], writes=[pg[:, :]])
            for grp in range(3):
                pq = nxt("psA", psA)
                for kvh in range(4):
                    c0 = (grp * 4 + kvh) * 128
                    TR(pq[:, kvh * 128:(kvh + 1) * 128], pg[:, c0:c0 + 128], idf[:, :])
                CP(sg[:, grp * 4:(grp + 1) * 4, (n % 4) * 128:(n % 4 + 1) * 128], pq[:, :].rearrange("p (k r) -> p k r", k=4))
            vs_ = vst[n % 2]
            CP(vs_[:, :], pg[:, 1536:2048])
            DMA(CVS[n * 128:(n + 1) * 128, :], vs_[:, :])
            if n % 4 == 3:
                cc = (n // 4) * 512
                DMA(CXT[:, cc:cc + 512].rearrange("(k d) c -> d k c", d=128), sg[:, 0:8, :])
                DMA(CKST[:, cc:cc + 512].rearrange("(k d) c -> d k c", d=128), sg[:, 8:12, :])
        AB.reset()
        AFl.reset()
        kcT = AB.alloc(4, 1024)
        vc = AB.alloc(8, 4, 128)
        p0 = AB.p
        XTs = AB.alloc(PAST)
        w1b = AB.alloc(32, 128)
        w2b = AB.alloc(128)
        peTb = AB.alloc(32)
        hg = AB.alloc(1024)
        cols = AFl.alloc(8)
        tmpf = AFl.alloc(128)
        b2row = AFl.alloc(128)
        for kvi in range(2):
            compress_setup(kvi, w1b, w2b, peTb, cols, tmpf)
            if kvi == 1:
                DMA(b2row[:, :], bass.AP(tensor=cmp_b2.tensor, offset=128, ap=[[0, 128], [1, 128]]))
            for kvh in range(4):
                r0 = (kvi * 4 + kvh) * 128
                for q in range(4):
                    DMA(XTs[:, q * 4096:(q + 1) * 4096], CXT[r0:r0 + 128, q * 4096:(q + 1) * 4096])
                for (c0, n) in ((0, 512), (512, 511)):
                    ps = nxt("psA", psA)
                    for r in range(32):
                        MM(ps[:, :n], lhsT=w1b[:, r, :], rhs=sub_ap(XTs, r + 16 * c0, 16, n), start=(r == 0), stop=(r == 31))
                    ACT(hg[:, c0:c0 + n], ps[:, :n], AF.Gelu_apprx_tanh, bias=cols[:, 0:1], scale=1.0)
                if kvi == 0:
                    for (c0, n) in ((0, 512), (512, 511)):
                        pq = nxt("psO", psO)
                        MM(pq[:, :n], lhsT=w2b[:, :], rhs=hg[:, c0:c0 + n], start=True, stop=True)
                        ACT(kcT[:, kvh, c0:c0 + n], pq[:, :n], AF.Identity, bias=cols[:, 1:2], scale=1.0)
                else:
                    for ct in range(8):
                        kc = min(128, 1023 - ct * 128)
                        pq = nxt("psO", psO)
                        MM(pq[:kc, :128], lhsT=hg[:, ct * 128:ct * 128 + kc], rhs=w2b[:, :], start=True, stop=True)
                        TT(vc[:kc, ct, kvh, :], pq[:kc, :128], b2row[:kc, :], ALU.add)
        AB.p = p0
        AFl.reset()
        KsT = AB.alloc(16400)
        Vsel = AB.alloc(128, 128)
        vnew = AB.alloc(2, 128)
        qts = AB.alloc(16)
        pbh = AB.alloc(8704)
        ptsh = AB.alloc(68, 16)
        kwT = AB.alloc(640)
        vw = AB.alloc(4, 128)
        Pb2 = AB.alloc(640)
        PTs2 = AB.alloc(5, 128)
        o16b = AB.alloc(128)
        ssh = AFl.alloc(8208)
        bch = [AFl.alloc(516)]
        pcp = AFl.alloc(1040)
        wkA = AFl.alloc(264)
        t1b = AFl.alloc(264)
        pslb = AFl.alloc(264)
        scb = AFl.alloc(264)
        sc2b = AFl.alloc(264)
        mnegb = AFl.alloc(264)
        mneg16b = AFl.alloc(264)
        o16 = AFl.alloc(128)
        ohs = [AFl.alloc(128) for _ in range(2)]
        sAs = AFl.alloc(264)
        sBs = AFl.alloc(264)
        kwf = ssh[:, 5120:5632].rearrange("p (a b) -> p a b", a=4)
        Ssb2 = ssh[:, 6144:6784]
        tmpO = ssh[:, 7168:7296]
        DMA(sAs[0:4, :], c_selAs[:, :])
        DMA(sBs[0:4, :], c_selBs[:, :])
        MEMSET(pcp[0:4, :], 0.0)
        MEMSET(pslb[0:4, :], 0.0)
        for kvh in range(4):
            for q in range(4):
                DMA(KsT[:, q * 4096:(q + 1) * 4096], CKST[kvh * 128:(kvh + 1) * 128, q * 4096:(q + 1) * 4096])
            DMA(KsT[:, PAST:PAST + 4], KST[kvh * 128:(kvh + 1) * 128, S:S + 4])
            for q in range(8):
                DMA(Vsel[:, q * 16:(q + 1) * 16, :], CVS[q * 2048:(q + 1) * 2048, kvh * 128:(kvh + 1) * 128].rearrange("(n p) c -> p n c", p=128))
            DMA(vnew[0:4, 0, :], VS1[S:S + 4, kvh * 128:(kvh + 1) * 128])
            DMA(vnew[0:4, 1, :], VW1[S:S + 4, kvh * 128:(kvh + 1) * 128])
            DMA(qts[:, :].rearrange("d (g t) -> d g t", g=4), QT1[kvh * 512:(kvh + 1) * 512, S:S + 4].rearrange("(g d) t -> d g t", d=128))
            gt16 = wkA[0:16, 0:3]
            for g in range(4):
                for br in range(3):
                    col = br * 16 + kvh * 4 + g
                    DMA(wkA[4 * g:4 * g + 4, br:br + 1], G1[S:S + 4, col:col + 1], allow_slow_non_contiguous=True)
            ssc = ssh[0:16, 0:1024]
            pf = ssh[0:16, 1024:2048]
            bsm = ssh[0:16, 2048:3072]
            for g in range(4):
                DMA(ssh[4 * g:4 * g + 4, 2048:2048 + 1023], tb_ap("cmps", kvh * 4 + g, 0, 1024, 4, 1023))
            for (c0, n) in ((0, 512), (512, 511)):
                ps = nxt("psA", psA)
                MM(ps[:16, :n], lhsT=qts[:, :], rhs=kcT[:, kvh, c0:c0 + n], start=True, stop=True)
                STT(ssc[:, c0:c0 + n], ps[:16, :n], SCALE, bsm[:, c0:c0 + n], ALU.mult, ALU.add)
            nm = wkA[0:16, 4:5]
            l_ = wkA[0:16, 5:6]
            rl = wkA[0:16, 6:7]
            RED(nm, ssc[:, :1023], ALU.max, negate=True)
            ACT(pf[:, :1023], ssc[:, :1023], AF.Exp, bias=nm, scale=1.0)
            RED(l_, pf[:, :1023], ALU.add)
            RECIP(rl, l_)
            TS(pf[:, :1023], pf[:, :1023], rl, ALU.mult)
            for (c0, n) in ((0, 512), (512, 511)):
                pq = nxt("psO", psO)
                MM(pq[:4, :n], lhsT=selm_sb[:, :], rhs=pf[:, c0:c0 + n], start=True, stop=True)
                CP(pcp[0:4, 1 + c0:1 + c0 + n], pq[:4, :n], eng="dve")
            CP(pbh[0:16, :1023], pf[:, :1023], eng="act")
            pt = nxt("psT", psT)
            for j in range(8):
                kc = min(128, 1023 - j * 128)
                TR(pt[:kc, j * 16:(j + 1) * 16], pbh[0:16, j * 128:j * 128 + kc], idb[0:16, 0:16])
            CP(ptsh[:, 0:7, :], pt[:, 0:112].rearrange("p (j q) -> p j q", q=16), eng="dve")
            CP(ptsh[:127, 7, :], pt[:127, 112:128], eng="dve")
            po = nxt("psO", psO)
            for j in range(8):
                kc = min(128, 1023 - j * 128)
                MM(po[:16, :128], lhsT=ptsh[:kc, j, :], rhs=vc[:kc, j, kvh, :], start=(j == 0), stop=(j == 7))
            ACT(o16[0:16, :], po[:16, :128], AF.Copy, scale=gt16[:, 0:1])
            va = pcp[0:4, 0:1028].rearrange("p (j f) -> p j f", f=4)
            vb_ = pcp[0:4, 4:1032].rearrange("p (j f) -> p j f", f=4)
            t1 = t1b[0:4, 0:257]
            psl = pslb[0:4, 0:257]
            sc = scb[0:4, :]
            sc2 = sc2b[0:4, :]
            m8 = wkA[0:4, 8:16]
            m8b = wkA[0:4, 16:24]
            mneg = mnegb[0:4, :]
            mneg16 = mneg16b[0:16, :]
            TT(t1, va[:, :, 1], va[:, :, 2], ALU.add)
            TT(t1, t1, va[:, :, 3], ALU.add)
            STT(psl, t1, 2.0, va[:, :, 0], ALU.mult, ALU.add)
            TT(psl, psl, vb_[:, :, 0], ALU.add)
            TT(sc, pslb[0:4, :], sAs[0:4, :], ALU.mult)
            TT(sc, sc, sBs[0:4, :], ALU.add)
            P.op("dve", lambda e, m8=m8, sc=sc: e.max(out=m8, in_=sc), reads=[sc], writes=[m8])
            P.op("dve", lambda e, m8=m8, sc=sc, sc2=sc2: e.match_replace(out=sc2, in_to_replace=m8, in_values=sc, imm_value=-1e9),
                 reads=[sc, m8], writes=[sc2])
            P.op("dve", lambda e, m8b=m8b, sc2=sc2: e.max(out=m8b, in_=sc2), reads=[sc2], writes=[m8b])
            TS(mneg, sc, m8b[:, 7:8], ALU.is_ge)
            TS(mneg, mneg, -1.0, ALU.add, -NEG, ALU.mult)
            pq = nxt("psO", psO)
            MM(pq[:16, :264], lhsT=selmT_sb[:, :], rhs=mneg, start=True, stop=True)
            CP(mneg16, pq[:16, :264], eng="dve")
            for half in range(2):
                cl = list(range(0, 16)) if half == 0 else list(range(16, 33))
                for lc, c in enumerate(cl):
                    n = 512 if c < 32 else 4
                    ps = nxt("psA", psA)
                    MM(ps[:16, :n], lhsT=qts[:, :], rhs=KsT[:, c * 512:c * 512 + n], start=True, stop=True)
                    bc_ = nxt("bch", bch)
                    for g in range(4):
                        DMA(bc_[4 * g:4 * g + 4, :n], tb_ap("sels", kvh * 4 + g, c * 512, 16512, 4, n))
                    STT(ssh[0:16, lc * 512:lc * 512 + n], ps[:16, :n], SCALE, bc_[0:16, :n], ALU.mult, ALU.add)
                    if n == 512:
                        v3 = ssh[0:16, lc * 512:(lc + 1) * 512].rearrange("p (j s) -> p j s", s=64)
                        TT(v3, v3, mneg16[:, c * 8:c * 8 + 8].unsqueeze(2).to_broadcast([16, 8, 64]), ALU.add)
                    else:
                        v3 = ssh[0:16, lc * 512:lc * 512 + 4]
                        TT(v3, v3, mneg16[:, 256:257].to_broadcast([16, 4]), ALU.add)
                nkh = 16 * 512 if half == 0 else 16 * 512 + 4
                nmh = wkA[0:16, 24 + half:25 + half]
                lh = wkA[0:16, 26 + half:27 + half]
                RED(nmh, ssh[0:16, :nkh], ALU.max, negate=True)
                ACT(pbh[0:16, :nkh], ssh[0:16, :nkh], AF.Exp, bias=nmh, scale=1.0)
                RED(lh, pbh[0:16, :nkh], ALU.add)
                nch = (nkh + 127) // 128
                for j0 in range(0, nch, 64):
                    j1 = min(nch, j0 + 64)
                    pt = nxt("psT", psT)
                    for j in range(j0, j1):
                        kc = min(128, nkh - j * 128)
                        TR(pt[:kc, (j - j0) * 16:(j - j0 + 1) * 16], pbh[0:16, j * 128:j * 128 + kc], idb[0:16, 0:16])
                    nfull = sum(1 for j in range(j0, j1) if nkh - j * 128 >= 128)
                    if nfull:
                        CP(ptsh[:, j0:j0 + nfull, :], pt[:, 0:nfull * 16].rearrange("p (j q) -> p j q", q=16), eng="dve")
                    if nfull < j1 - j0:
                        kc = nkh - (j0 + nfull) * 128
                        CP(ptsh[:kc, j0 + nfull, :], pt[:kc, nfull * 16:(nfull + 1) * 16], eng="dve")
                po = nxt("psO", psO)
                for j in range(nch):
                    kc = min(128, nkh - j * 128)
                    pg_ = (0 if half == 0 else 64) + j
                    rhs = Vsel[:, pg_, :] if pg_ < 128 else vnew[0:4, 0, :]
                    MM(po[:16, :128], lhsT=ptsh[:kc, j, :], rhs=rhs, start=(j == 0), stop=(j == nch - 1))
                CP(ohs[half][0:16, :], po[:16, :128], eng="act")
            nmn = wkA[0:16, 28:29]
            e0 = wkA[0:16, 29:30]
            e1 = wkA[0:16, 30:31]
            den = wkA[0:16, 31:32]
            TT(nmn, wkA[0:16, 24:25], wkA[0:16, 25:26], ALU.min)
            TT(e0, nmn, wkA[0:16, 24:25], ALU.subtract)
            TT(e1, nmn, wkA[0:16, 25:26], ALU.subtract)
            ACT(e0, e0, AF.Exp)
            ACT(e1, e1, AF.Exp)
            TT(den, e0, wkA[0:16, 26:27], ALU.mult)
            STT(den, e1, wkA[0:16, 27:28], den, ALU.mult, ALU.add)
            RECIP(den, den)
            TT(den, den, gt16[:, 1:2], ALU.mult)
            TT(e0, e0, den, ALU.mult)
            TT(e1, e1, den, ALU.mult)
            STT(o16[0:16, :], ohs[0][0:16, :], e0, o16[0:16, :], ALU.mult, ALU.add)
            STT(o16[0:16, :], ohs[1][0:16, :], e1, o16[0:16, :], ALU.mult, ALU.add)
            for kv in range(2):
                DMA(kwf[:, :, :], bass.AP(tensor=stwin.tensor, offset=kv * 512 + kvh * 128, ap=[[1024, 128], [128 * 1024, 4], [1, 128]]))
                if kv == 0:
                    pq = nxt("psA", psA)
                    for ti in range(4):
                        TR(pq[:, ti * 128:(ti + 1) * 128], kwf[:, ti, :], idf[:, :])
                    CP(kwT[:, 0:512], pq[:, :])
                else:
                    CP(vw[:, :, :], kwf[:, :, :])
            DMA(kwT[:, 512:516], KWT[kvh * 128:(kvh + 1) * 128, S:S + 4])
            bw = bch[0]
            for g in range(4):
                DMA(ssh[4 * g:4 * g + 4, 4096:4096 + 516], tb_ap("wins", kvh * 4 + g, 0, 516, 4, 516))
            chunks = []
            for (c0, n) in ((0, 512), (512, 4)):
                ps = nxt("psA", psA)
                MM(ps[:16, :n], lhsT=qts[:, :], rhs=kwT[:, c0:c0 + n], start=True, stop=True)
                chunks.append((ps[:16, :n], c0, n))
            softmax_pv(16, chunks, 516, ssh[0:16, 4096:4096 + 516],
                       lambda j: ((vw[:, j, :], 128) if j < 4 else (vnew[0:4, 1, :], 4)),
                       tmpO[0:16, :], wkA[0:16, 32:33], wkA[0:16, 33:34], Ssb2, Pb2, PTs2)
            STT(o16[0:16, :], tmpO[0:16, :], gt16[:, 2:3], o16[0:16, :], ALU.mult, ALU.add)
            CP(o16b[0:16, :], o16[0:16, :], eng="act")
            for g in range(4):
                DMA(OM[S:S + 4, (kvh * 4 + g) * 128:(kvh * 4 + g + 1) * 128], o16b[4 * g:4 * g + 4, :])

    nsa_prompt()
    if upto <= 7:
        zt = arb_t[0:4, 0:2048]
        MEMSET(zt, 0.0)
        DMA(OM[S:S + 4, :], zt)
    else:
        nsa_sample()
    mixer_out(w_out_b, HA, HB)
    ffn(1, HB, HA)
    ple(1, HA, HB)
    final_norm(HB)
    return finish(nc, P, st)


def finish(nc, P, st):
    P.finalize(st)
    if os.environ.get('KVERBOSE'):
        print('ops/sems', P.counts, P.nwaits, flush=True)
    with nc.Block() as block:
        P.emit(block)
    st.close()
    return nc


def make_inputs(core, I, consts):
    b = core % 4
    c = core
    f = np.float32
    m = {}
    m["x"] = np.concatenate([I["x_prompt"][b], I["x_sample"][c]], axis=0).astype(f)
    m["pp"] = np.concatenate([I["p_prompt"][:, b], I["p_sample"][:, c]], axis=1).astype(f)
    m["st128"] = I["state_dil_w128"][0, c].reshape(128, 2, D)
    m["st512"] = I["state_dil_w512"][0, c].reshape(512, 2, D)
    m["st2048"] = I["state_dil_w2048"][0, c].reshape(2048, 2, D)
    m["stwin"] = I["state_nsa_win"][0, c].reshape(512, 2, 512)
    m["stconv"] = np.ascontiguousarray(I["state_conv"][:, c])
    m["cache"] = I["cache_nsa_kv"][0].reshape(1280 * 128, 2048)
    if UPTO_DEFAULT < 8:
        m["cache"] = m["cache"][:128]
    m["ptab"] = np.ascontiguousarray(I["page_table"][c:c + 1]).astype(np.int32)
    m["relb"] = I["rel_bias"]
    m["norms"] = np.concatenate([I["norm_mix"][0:1], I["norm_ffn"][0:1], I["norm_ple"][0:1],
                                 I["norm_mix"][1:2], I["norm_ffn"][1:2], I["norm_ple"][1:2],
                                 I["norm_final"][None, :]], axis=0)
    m["w_in_a"] = I["w_in_a"][0]
    m["w_out_a"] = I["w_out_a"][0]
    m["w_in_b"] = I["w_in_b"][0]
    m["w_out_b"] = I["w_out_b"][0]
    m["cmp_pe"] = I["cmp_pe"][0]
    m["cmp_w1"] = I["cmp_w1"][0]
    m["cmp_b1"] = I["cmp_b1"][0]
    m["cmp_w2"] = I["cmp_w2"][0]
    m["cmp_b2"] = I["cmp_b2"][0]
    m["w_ffn_in"] = I["w_ffn_in"]
    m["conv_w"] = I["conv_w"]
    m["conv_b"] = I["conv_b"]
    m["w_ffn_out"] = I["w_ffn_out"]
    m["w_ple_gate"] = I["w_ple_gate"]
    m["w_ple_proj"] = I["w_ple_proj"]
    m.update(consts)
    return {k: np.ascontiguousarray(v) for k, v in m.items()}


def make_consts():
    tab, _ = host_tables()
    A, Bv, As, Bs = sel_tables()
    selm = np.zeros((16, 4), np.float32)
    for g in range(4):
        for t in range(4):
            selm[4 * g + t, t] = 1.0
    return {
        "c_tab": tab,
        "c_idf": np.eye(128, dtype=np.float32),
        "c_idb": np.eye(128, dtype=np.float32).astype(ml_dtypes.bfloat16),
        "c_selA": A, "c_selB": Bv, "c_selAs": As, "c_selBs": Bs,
        "c_rowv": (np.arange(S) >= 31).astype(np.float32)[:, None],
        "c_selm": selm,
        "c_iota": np.arange(128, dtype=np.float32)[:, None],
    }


def assemble(res):
    f = np.float32
    R = res
    y_p = np.stack([R[b]["o_y"][:S] for b in range(4)])
    y_s = np.stack([R[c]["o_y"][S:] for c in range(8)])
    outs = [y_p, y_s]
    for gi, (w, _) in enumerate(DIL):
        kp = min(w, S)
        outs.append(np.stack([R[b]["o_d%dp" % w].reshape(kp, 2, H, DH) for b in range(4)])[None])
        outs.append(np.stack([R[c]["o_d%ds" % w].reshape(w, 2, H, DH) for c in range(8)])[None])
    outs.append(np.stack([R[b]["o_winp"].reshape(512, 2, 4, DH) for b in range(4)])[None])
    outs.append(np.stack([R[c]["o_wins"].reshape(512, 2, 4, DH) for c in range(8)])[None])
    outs.append(np.stack([R[b]["o_convp"] for b in range(4)], axis=1))
    outs.append(np.stack([R[c]["o_convs"] for c in range(8)], axis=1))
    outs.append(np.stack([R[b]["o_kvp"].reshape(S, 4, 4, DH) for b in range(4)])[None])
    outs.append(np.stack([R[c]["o_kvs"].reshape(4, 4, 4, DH) for c in range(8)])[None])
    return tuple(np.ascontiguousarray(o.astype(f)) for o in outs)


def kernel(**inputs):
    I = {k: np.asarray(v) for k, v in inputs.items()}
    nc = build()
    consts = make_consts()
    in_maps = [make_inputs(c, I, consts) for c in range(8)]
    res = run_bass_kernel_spmd(nc, in_maps, core_ids=list(range(8)))
    return assemble(res.results)
```

```python
import math
import os
from contextlib import ExitStack
import numpy as np
import ml_dtypes
import concourse.bass as bass
import concourse.mybir as mybir
from concourse.bass_utils import run_bass_kernel_spmd
F32 = mybir.dt.float32
BF16 = mybir.dt.bfloat16
I32 = mybir.dt.int32
U32 = mybir.dt.uint32
AF = mybir.ActivationFunctionType
ALU = mybir.AluOpType
AX = mybir.AxisListType

NDMA_SEM = 6
SAME_ENGINE_SYNC = True


def _region(ap):
    t = ap.tensor
    shape = list(t.shape)
    rowsize = 1
    for s in shape[1:]:
        rowsize *= s
    nrows = shape[0]
    if t.name.startswith("ps"):
        return (t.name, 0, nrows, 0, rowsize)
    off = ap.offset
    r0 = off // rowsize
    c0 = off % rowsize
    rlo = rhi = r0
    clo = chi = c0
    for step, cnt in ap.ap:
        if cnt <= 1 or step == 0:
            continue
        ext = step * (cnt - 1)
        if step % rowsize == 0:
            e = ext // rowsize
            if e > 0:
                rhi += e
            else:
                rlo += e
        else:
            if ext > 0:
                chi += ext
            else:
                clo += ext
    if clo < 0 or chi >= rowsize:
        return (t.name, 0, nrows, 0, rowsize)
    return (t.name, rlo, rhi + 1, clo, chi + 1)


def _overlap(a, b):
    return a[1] < b[2] and b[1] < a[2] and a[3] < b[4] and b[3] < a[4]


def _contains(a, b):
    return a[1] <= b[1] and b[2] <= a[2] and a[3] <= b[3] and b[4] <= a[4]


class Op:
    __slots__ = ("idx", "eng", "fn", "is_dma", "deps", "signal", "sem", "semval", "waits", "nosync_same")

    def __init__(self, idx, eng, fn, is_dma):
        self.idx = idx
        self.eng = eng
        self.fn = fn
        self.is_dma = is_dma
        self.deps = set()
        self.signal = False
        self.sem = None
        self.semval = 0
        self.waits = []
        self.nosync_same = False


class Prog:
    ENGS = ("sync", "act", "dve", "pool", "pe")

    def __init__(self, nc, serial=False):
        self.nc = nc
        self.ops = []
        self.acc = {}
        self.serial = serial

    def _track(self, op, reads, writes):
        for ap in reads:
            reg = _region(ap)
            lst = self.acc.setdefault(reg[0], [])
            is_ps = reg[0].startswith("ps")
            for (r, i, w) in lst:
                if w and _overlap(r, reg):
                    op.deps.add(i)
                elif is_ps and not w and self.ops[i].eng != op.eng:
                    op.deps.add(i)
        for ap in writes:
            reg = _region(ap)
            lst = self.acc.setdefault(reg[0], [])
            for (r, i, w) in lst:
                if _overlap(r, reg):
                    op.deps.add(i)
        for ap in reads:
            reg = _region(ap)
            self.acc[reg[0]].append((reg, op.idx, False))
        for ap in writes:
            reg = _region(ap)
            lst = self.acc[reg[0]]
            lst[:] = [x for x in lst if not _contains(reg, x[0])]
            lst.append((reg, op.idx, True))
        op.deps.discard(op.idx)
        best = {}
        keep = set()
        for i in op.deps:
            d = self.ops[i]
            if d.is_dma:
                keep.add(i)
            elif best.get(d.eng, -1) < i:
                best[d.eng] = i
        keep.update(best.values())
        op.deps = keep

    def op(self, eng, fn, reads=(), writes=(), pe_acc=False):
        o = Op(len(self.ops), eng, fn, False)
        o.nosync_same = pe_acc
        self.ops.append(o)
        self._track(o, reads, writes)
        return o

    def dma(self, eng, out, in_, extra_reads=(), **kw):
        def fn(e, out=out, in_=in_, kw=kw):
            return e.dma_start(out=out, in_=in_, **kw)
        o = Op(len(self.ops), eng, fn, True)
        self.ops.append(o)
        self._track(o, [in_] + list(extra_reads), [out])
        return o

    def dma_custom(self, eng, fn, reads, writes):
        o = Op(len(self.ops), eng, fn, True)
        self.ops.append(o)
        self._track(o, reads, writes)
        return o

    def finalize(self, stack):
        nc = self.nc
        ops = self.ops
        if self.serial:
            for i, o in enumerate(ops):
                o.deps = {i - 1} if i > 0 else set()
        for o in ops:
            for d in o.deps:
                dop = ops[d]
                if dop.is_dma:
                    continue
                if dop.eng == o.eng and (dop.eng == "pe" or not SAME_ENGINE_SYNC):
                    continue
                dop.signal = True
        csem = {e: stack.enter_context(nc.semaphore("c_" + e)) for e in self.ENGS}
        dsem = {e: [stack.enter_context(nc.semaphore("d_%s%d" % (e, k))) for k in range(NDMA_SEM)]
                for e in ("sync", "pool", "act")}
        ccount = {e: 0 for e in self.ENGS}
        dcount = {e: 0 for e in dsem}
        seen = {e: {} for e in self.ENGS}
        semobj = {}
        last_dma_vals = {}
        for o in ops:
            waits = {}

            def need(sem, val):
                k = id(sem)
                semobj[k] = sem
                if seen[o.eng].get(k, 0) >= val:
                    return
                if waits.get(k, 0) < val:
                    waits[k] = val
            for d in sorted(o.deps):
                dop = ops[d]
                if dop.is_dma:
                    need(dop.sem, dop.semval)
                else:
                    if not dop.signal:
                        continue
                    if dop.eng == o.eng and (dop.eng == "pe" or not SAME_ENGINE_SYNC):
                        continue
                    need(dop.sem, dop.semval)
            if o.is_dma:
                n = dcount[o.eng]
                dcount[o.eng] += 1
                k = n % NDMA_SEM
                o.sem = dsem[o.eng][k]
                o.semval = 16 * (n // NDMA_SEM + 1)
                if n >= NDMA_SEM:
                    need(o.sem, o.semval - 16)
                last_dma_vals[id(o.sem)] = (o.sem, o.semval)
            elif o.signal:
                ccount[o.eng] += 1
                o.sem = csem[o.eng]
                o.semval = ccount[o.eng]
            for k, v in waits.items():
                seen[o.eng][k] = v
            o.waits = [(semobj[k], v) for k, v in waits.items()]
        self.final_waits = list(last_dma_vals.values())
        self.nwaits = sum(len(o.waits) for o in ops)
        self.counts = (dict(ccount), dict(dcount), len(ops))

    def emit(self, block):
        ops = self.ops
        final_waits = self.final_waits

        def run(eng_name, e):
            for o in ops:
                if o.eng != eng_name:
                    continue
                for (s, v) in o.waits:
                    e.wait_ge(s, v)
                ins = o.fn(e)
                if o.is_dma:
                    ins.then_inc(o.sem, 16)
                elif o.signal:
                    ins.then_inc(o.sem, 1)
            if eng_name == "sync":
                for (s, v) in final_waits:
                    e.wait_ge(s, v)

        @block.sync
        def _(e):
            run("sync", e)

        @block.scalar
        def _(e):
            run("act", e)

        @block.vector
        def _(e):
            run("dve", e)

        @block.gpsimd
        def _(e):
            run("pool", e)

        @block.tensor
        def _(e):
            run("pe", e)

S = 2048
D = 2048
NT = 2052
H = 16
DH = 128
DFF = 5632
PLE = 256
SCALE = float(DH ** -0.5)
NEG = -30000.0
EPS = 1e-6
DIL = ((128, 1), (512, 4), (2048, 16))
PAST = 16384
NPAGE = 128
NSB_S = 257
TILES = [(i, i * 128, 128) for i in range(16)] + [(16, 2048, 4)]


def _bucket(dist):
    dist = np.maximum(np.asarray(dist, np.int64), 0)
    me = 16
    df = np.maximum(dist, 1).astype(np.float32)
    large = me + (np.log(df / np.float32(me)) / np.float32(math.log(2048 / me)) * np.float32(32 - me)).astype(np.int32)
    large = np.minimum(large, 31)
    return np.where(dist < me, dist, large).astype(np.int64)


def _onehot(dist, valid):
    dist = np.asarray(dist).reshape(-1)
    valid = np.asarray(valid).reshape(-1)
    n = dist.shape[0]
    t = np.zeros((33, n), np.float32)
    b = _bucket(dist)
    idx = np.nonzero(valid)[0]
    t[b[idx], idx] = 1.0
    t[32, ~valid] = NEG
    return t


def host_tables():
    tabs = []
    offs = {}
    pos = 0

    def add(name, dist, valid):
        nonlocal pos
        t = _onehot(dist, valid)
        n = t.shape[1]
        npad = (-n) % 512
        if npad:
            t = np.concatenate([t, np.zeros((33, npad), np.float32)], axis=1)
        offs[name] = (pos, n)
        tabs.append(t)
        pos += t.shape[1]
    for g, (win, dil) in enumerate(DIL):
        u = np.arange(383)
        d = 255 - u
        add("l0p%d" % g, d * dil, (d >= 0) & (d <= 128))
    t4 = np.arange(4)[:, None]
    i = np.arange(132)[None, :]
    d = 128 + t4 - i
    add("l0s0", d, (d >= 0) & (d <= 128))
    i = np.arange(516)[None, :]
    d = 512 + t4 - i
    add("l0s1", d, (d >= 0) & (d <= 512) & (d % 4 == 0))
    key = np.arange(516)[None, :]
    tp = np.where(key < 512, key // 128, key - 512)
    a = np.where(key < 512, key % 128, 128)
    d = 2048 + t4 - 16 * a - tp
    add("l0s2", d, (tp == t4) & (d >= 0))
    qi = np.arange(128)[:, None]
    z = np.arange(247)[None, :]
    d = qi - 16 * z + 1889
    add("cmpp", d, d >= 0)
    u = np.arange(2175)
    d = 2047 - u
    add("selp", d, d >= 0)
    u = np.arange(767)
    d = 639 - u
    add("winp", d, (d >= 0) & (d <= 511))
    c = np.arange(1024)[None, :]
    d = PAST + t4 - 16 * c - 31
    add("cmps", d, (d >= 0) & (c < 1023))
    k = np.arange(16512)[None, :]
    d = PAST + t4 - k
    add("sels", d, (d >= 0) & (k < PAST + 4))
    i = np.arange(516)[None, :]
    d = 512 + t4 - i
    add("wins", d, (d >= 0) & (d <= 511))
    tab = np.concatenate(tabs, axis=1)
    return tab, offs


def sel_tables():
    t = np.arange(S)[:, None]
    jb = np.arange(32)[None, :]
    cur = t // 64
    valid = jb <= cur
    forced = (jb == 0) | (jb == cur) | (jb == cur - 1)
    A = (valid & ~forced).astype(np.float32)
    Bv = np.where(forced, 1e4, np.where(valid, 0.0, -1.0)).astype(np.float32)
    jb = np.arange(264)[None, :]
    cur = np.full((4, 1), (PAST // 64))
    valid = jb <= cur
    forced = (jb == 0) | (jb == cur) | (jb == cur - 1)
    As = np.broadcast_to((valid & ~forced), (4, 264)).astype(np.float32)
    Bs = np.broadcast_to(np.where(forced, 1e4, np.where(valid, 0.0, -1.0)), (4, 264)).astype(np.float32)
    return A, Bv, np.ascontiguousarray(As), np.ascontiguousarray(Bs)

class Arena:
    def __init__(self, t, n):
        self.t = t
        self.n = n
        self.p = 0

    def reset(self):
        self.p = 0

    def alloc(self, *shape):
        n = 1
        for s in shape:
            n *= s
        n2 = (n + 15) // 16 * 16
        assert self.p + n2 <= self.n, ("arena overflow", self.p, n2, self.n)
        v = self.t[:, self.p:self.p + n]
        self.p += n2
        if len(shape) == 2:
            v = v.rearrange("p (a b) -> p a b", a=shape[0])
        elif len(shape) == 3:
            v = v.rearrange("p (a b c) -> p a b c", a=shape[0], b=shape[1])
        return v


NTAB_COLS = None


UPTO_DEFAULT = 99


def build(upto=None, serial=False):
    if upto is None:
        upto = UPTO_DEFAULT
    tab, toffs = host_tables()
    NTAB = tab.shape[1]
    nc = bass.Bass("TRN2", target_bir_lowering=False)

    def din(name, shape, dt=F32):
        return nc.dram_tensor(name, list(shape), dt, kind="ExternalInput").ap()

    def dout(name, shape, dt=F32):
        return nc.dram_tensor(name, list(shape), dt, kind="ExternalOutput").ap()

    def dscr(name, shape, dt=F32):
        return nc.dram_tensor(name, list(shape), dt, kind="Internal").ap()

    x = din("x", [NT, D])
    pp = din("pp", [2, NT, PLE])
    st128 = din("st128", [128, 2, D])
    st512 = din("st512", [512, 2, D])
    st2048 = din("st2048", [2048, 2, D])
    stwin = din("stwin", [512, 2, 512])
    stconv = din("stconv", [2, 2, DFF])
    cache = din("cache", [(1 if (os.environ.get("KSMALL") or upto < 8) else 1280) * 128, 2048])
    ptab = din("ptab", [1, 128], I32)
    relb = din("relb", [32, 16])
    norms = din("norms", [7, D])
    w_in_a = din("w_in_a", [D, 18432])
    w_out_a = din("w_out_a", [D, D])
    w_in_b = din("w_in_b", [D, 5168])
    w_out_b = din("w_out_b", [D, D])
    cmp_pe = din("cmp_pe", [2, 32, 128])
    cmp_w1 = din("cmp_w1", [2, 32, 128, 128])
    cmp_b1 = din("cmp_b1", [2, 128])
    cmp_w2 = din("cmp_w2", [2, 128, 128])
    cmp_b2 = din("cmp_b2", [2, 128])
    w_ffn_in = din("w_ffn_in", [2, D, 2 * DFF])
    conv_w = din("conv_w", [2, 3, DFF])
    conv_b = din("conv_b", [2, DFF])
    w_ffn_out = din("w_ffn_out", [2, DFF, D])
    w_ple_gate = din("w_ple_gate", [2, D, D])
    w_ple_proj = din("w_ple_proj", [2, PLE, D])
    c_tab = din("c_tab", [33, NTAB])
    c_idf = din("c_idf", [128, 128])
    c_idb = din("c_idb", [128, 128], BF16)
    c_selA = din("c_selA", [S, 32])
    c_selB = din("c_selB", [S, 32])
    c_selAs = din("c_selAs", [4, 264])
    c_selBs = din("c_selBs", [4, 264])
    c_rowv = din("c_rowv", [S, 1])
    c_selm = din("c_selm", [16, 4])
    c_iota = din("c_iota", [128, 1])
    o_y = dout("o_y", [NT, D])
    o_dp = [dout("o_d%dp" % w, [min(w, S), 2, D]) for (w, _) in DIL]
    o_ds = [dout("o_d%ds" % w, [w, 2, D]) for (w, _) in DIL]
    o_winp = dout("o_winp", [512, 2, 512])
    o_wins = dout("o_wins", [512, 2, 512])
    o_convp = dout("o_convp", [2, 2, DFF])
    o_convs = dout("o_convs", [2, 2, DFF])
    o_kvp = dout("o_kvp", [S, 2048])
    o_kvs = dout("o_kvs", [4, 2048])
    HA = dscr("HA", [NT, D])
    HB = dscr("HB", [NT, D])
    QT0 = dscr("QT0", [3 * 16 * 128, NT], BF16)
    KT0 = dscr("KT0", [3 * 16 * 128, NT], BF16)
    V0 = dscr("V0", [3, NT, D], BF16)
    OG = dscr("OG", [3, NT, D])
    ML = dscr("ML", [3, NT, 32])
    UT = dscr("UT", [DFF, NT], BF16)
    TB = dscr("TB", [16, NTAB])
    QT1 = dscr("QT1", [16 * 128, NT], BF16)
    XT1 = dscr("XT1", [8 * 128, NT], BF16)
    KST = dscr("KST", [4 * 128, NT], BF16)
    KWT = dscr("KWT", [4 * 128, NT], BF16)
    VS1 = dscr("VS1", [NT, 512], BF16)
    VW1 = dscr("VW1", [NT, 512], BF16)
    G1 = dscr("G1", [NT, 48])
    OM = dscr("OM", [NT, D], BF16)
    CXT = dscr("CXT", [8 * 128, PAST], BF16)
    CKST = dscr("CKST", [4 * 128, PAST], BF16)
    CVS = dscr("CVS", [PAST, 512], BF16)
    TZL = {"l0p0": 383, "l0p1": 383, "l0p2": 383, "selp": 2175, "winp": 767}
    TZ = {k: dscr("TZ_" + k, [16, 128, L]) for k, L in TZL.items()}

    st = ExitStack()
    NB_ELEMS = 60000
    NF_ELEMS = 13000
    arb_t = st.enter_context(nc.sbuf_tensor("arena_b", [128, NB_ELEMS], BF16))
    arf_t = st.enter_context(nc.sbuf_tensor("arena_f", [128, NF_ELEMS], F32))
    idf = st.enter_context(nc.sbuf_tensor("idf", [128, 128], F32))
    idb = st.enter_context(nc.sbuf_tensor("idb", [128, 128], BF16))
    small = st.enter_context(nc.sbuf_tensor("small", [128, 256], F32))
    rb33 = st.enter_context(nc.sbuf_tensor("rb33", [33, 16], F32))
    pt_i32 = st.enter_context(nc.sbuf_tensor("pt_i32", [128, 128], I32))
    idx_i32 = st.enter_context(nc.sbuf_tensor("idx_i32", [128, 128], I32))
    selm_sb = st.enter_context(nc.sbuf_tensor("selm_sb", [16, 4], F32))
    selmT_sb = st.enter_context(nc.sbuf_tensor("selmT_sb", [4, 16], F32))
    iota_sb = st.enter_context(nc.sbuf_tensor("iota_sb", [128, 1], F32))
    psA = [st.enter_context(nc.psum_tensor("psA%d" % i, [128, 512], F32)) for i in range(4)]
    psT = [st.enter_context(nc.psum_tensor("psT%d" % i, [128, 1024], BF16)) for i in range(2)]
    psO = [st.enter_context(nc.psum_tensor("psO%d" % i, [128, 512], F32)) for i in range(2)]
    AB = Arena(arb_t, NB_ELEMS)
    AFl = Arena(arf_t, NF_ELEMS)
    P = Prog(nc, serial=serial)
    cnt = {"evac": 0, "psA": 0, "psT": 0, "psO": 0, "q": 0}

    def ACT(out, in_, func=AF.Copy, bias=None, scale=None, extra_reads=()):
        kw = {}
        rd = [in_] + list(extra_reads)
        if bias is not None:
            kw["bias"] = bias
            if not isinstance(bias, (int, float)):
                rd.append(bias)
        if scale is not None:
            kw["scale"] = scale
            if not isinstance(scale, (int, float)):
                rd.append(scale)
        return P.op("act", lambda e: e.activation(out=out, in_=in_, func=func, **kw), reads=rd, writes=[out])

    def TT(out, a, b, op, eng="dve"):
        return P.op(eng, lambda e: e.tensor_tensor(out=out, in0=a, in1=b, op=op), reads=[a, b], writes=[out])

    def TS(out, a, s1, op0, s2=None, op1=None, eng="dve"):
        rd = [a]
        if not isinstance(s1, (int, float)):
            rd.append(s1)
        if s2 is not None and not isinstance(s2, (int, float)):
            rd.append(s2)
        if op1 is None:
            return P.op(eng, lambda e: e.tensor_scalar(out=out, in0=a, scalar1=s1, scalar2=None, op0=op0), reads=rd, writes=[out])
        return P.op(eng, lambda e: e.tensor_scalar(out=out, in0=a, scalar1=s1, scalar2=s2, op0=op0, op1=op1), reads=rd, writes=[out])

    def STT(out, a, s, b, op0, op1):
        rd = [a, b]
        if not isinstance(s, (int, float)):
            rd.append(s)
        return P.op("dve", lambda e: e.scalar_tensor_tensor(out=out, in0=a, scalar=s, in1=b, op0=op0, op1=op1), reads=rd, writes=[out])

    def CP(out, in_, eng=None):
        if eng is None:
            cnt["evac"] += 1
            eng = "act" if cnt["evac"] % 2 else "dve"
        if eng == "act":
            return ACT(out, in_)
        return P.op(eng, lambda e: e.tensor_copy(out=out, in_=in_), reads=[in_], writes=[out])

    def RED(out, in_, op, negate=False):
        return P.op("dve", lambda e: e.tensor_reduce(out=out, in_=in_, axis=AX.X, op=op, negate=negate), reads=[in_], writes=[out])

    def RECIP(out, in_):
        return P.op("dve", lambda e: e.reciprocal(out=out, in_=in_), reads=[in_], writes=[out])

    def MEMSET(ap, v, eng="dve"):
        return P.op(eng, lambda e: e.memset(ap, v), writes=[ap])

    def MM(out, lhsT, rhs, start, stop):
        return P.op("pe", lambda e: e.matmul(out, lhsT=lhsT, rhs=rhs, start=start, stop=stop), reads=[lhsT, rhs], writes=[out])

    def TR(out, in_, ident):
        return P.op("pe", lambda e: e.transpose(out, in_, ident), reads=[in_, ident], writes=[out])

    def DMA(out, in_, eng=None, **kw):
        if eng is None:
            eng = "sync"
        return P.dma(eng, out, in_, **kw)

    def nxt(key, lst):
        cnt[key] = cnt.get(key, 0) + 1
        return lst[cnt[key] % len(lst)]

    DMA(idf[:], c_idf[:, :])
    DMA(idb[:], c_idb[:, :])
    DMA(rb33[0:32, :], relb[:, :])
    MEMSET(rb33[32:33, :], 1.0)

    def build_tables():
        AFl.reset()
        tb_in = [AFl.alloc(4, 512) for _ in range(2)]
        tb_out = [AFl.alloc(4, 512) for _ in range(2)]
        nchunk = NTAB // 2048
        rem = NTAB - nchunk * 2048
        pieces = [(i * 2048, 2048) for i in range(nchunk)]
        if rem:
            pieces.append((nchunk * 2048, rem))
        for pi, (c0, n) in enumerate(pieces):
            ti = tb_in[pi % 2]
            to = tb_out[pi % 2]
            tiv = ti.rearrange("p a b -> p (a b)")
            tov = to.rearrange("p a b -> p (a b)")
            DMA(tiv[0:33, :n], c_tab[:, c0:c0 + n])
            for j in range(n // 512):
                ps = psA[j % 4]
                MM(ps[0:16, :], lhsT=rb33[:, :], rhs=tiv[0:33, j * 512:(j + 1) * 512], start=True, stop=True)
                CP(tov[0:16, j * 512:(j + 1) * 512], ps[0:16, :])
            DMA(TB[:, c0:c0 + n], tov[0:16, :n])

    if upto != -2:
        build_tables()
    if upto == -1:
        return finish(nc, P, st)

    def build_tz():
        for name, L in TZL.items():
            off0, n = toffs[name]
            for h in range(16):
                src = bass.AP(tensor=TB.tensor, offset=h * NTAB + off0, ap=[[0, 128], [1, L]])
                DMA(TZ[name][h, :, :], src)

    if upto != -2:
        build_tz()

    def tz_ap(name, h, extra, fcount):
        L = TZL[name]
        return bass.AP(tensor=TZ[name].tensor, offset=h * 128 * L + extra, ap=[[L - 1, 128], [1, fcount]])

    def tb_ap(name, h, extra_off, pstep, pcount, fcount):
        off0, n = toffs[name]
        base = TB[h:h + 1, 0:1]
        return bass.AP(tensor=base.tensor, offset=base.offset + off0 + extra_off, ap=[[pstep, pcount], [1, fcount]])

    def norm_stage(Hsrc, gidx, aT, p_layer=None, pT=None):
        xts = [AFl.alloc(2048) for _ in range(2)]
        gb = AFl.alloc(2048)
        junk = AFl.alloc(2048)
        ats = [AB.alloc(2048) for _ in range(2)]
        pts = AFl.alloc(256) if pT is not None else None
        ptb = AB.alloc(256) if pT is not None else None
        DMA(gb, bass.AP(tensor=norms.tensor, offset=gidx * D, ap=[[0, 128], [1, D]]))
        for (i, t0, nt) in TILES:
            xt = xts[i % 2]
            at = ats[i % 2]
            ss = small[:, (i % 2) * 4:(i % 2) * 4 + 1]
            rs = small[:, (i % 2) * 4 + 1:(i % 2) * 4 + 2]
            DMA(xt[:nt], Hsrc[t0:t0 + nt, :])
            P.op("act", lambda e, xt=xt, ss=ss, nt=nt: e.activation(out=junk[:nt], in_=xt[:nt], func=AF.Square, accum_out=ss[:nt]),
                 reads=[xt[:nt]], writes=[junk[:nt], ss[:nt]])
            TS(rs[:nt], ss[:nt], 1.0 / D, ALU.mult, EPS, ALU.add)
            ACT(rs[:nt], rs[:nt], AF.Sqrt)
            RECIP(rs[:nt], rs[:nt])
            STT(at[:nt], xt[:nt], rs[:nt], gb[:nt], ALU.mult, ALU.mult)
            for q in range(4):
                pt = nxt("psT", psT)
                for j in range(4):
                    c = q * 4 + j
                    TR(pt[:, j * 128:j * 128 + nt], at[:nt, c * 128:(c + 1) * 128], idb[:nt, :nt])
                CP(aT[:, q * 4:(q + 1) * 4, t0:t0 + nt],
                   pt[:, 0:512].rearrange("p (j t) -> p j t", j=4)[:, :, :nt])
            if pT is not None:
                DMA(pts[:nt], pp[p_layer, t0:t0 + nt, :])
                CP(ptb[:nt], pts[:nt], eng="dve")
                pt = nxt("psT", psT)
                for j in range(2):
                    TR(pt[:, j * 128:j * 128 + nt], ptb[:nt, j * 128:(j + 1) * 128], idb[:nt, :nt])
                CP(pT[:, :, t0:t0 + nt], pt[:, 0:256].rearrange("p (j t) -> p j t", j=2)[:, :, :nt])

    def wload(w2d, k0, kch, n0, ncols, buf):
        src = w2d[k0 * 128:(k0 + kch) * 128, n0:n0 + ncols].rearrange("(kc p) n -> p kc n", p=128)
        for k in range(0, kch, 2):
            k2 = min(kch, k + 2)
            DMA(buf[:, k:k2, :ncols], src[:, k:k2, :], eng="pool")

    def mm_tok(ps, aT, t0, nt, wbuf, kch, ncols):
        for k in range(kch):
            MM(ps[:nt, :ncols], lhsT=aT[:, k, t0:t0 + nt], rhs=wbuf[:, k, :ncols], start=(k == 0), stop=(k == kch - 1))

    def mm_feat(ps, aT, t0, ntok, wbuf, kch, j):
        for k in range(kch):
            MM(ps[:, :ntok], lhsT=wbuf[:, k, j * 128:(j + 1) * 128], rhs=aT[:, k, t0:t0 + ntok], start=(k == 0), stop=(k == kch - 1))

    TOKBLK = [(0, 512), (512, 512), (1024, 512), (1536, 512), (2048, 4)]

    def l0_proj():
        AB.reset()
        AFl.reset()
        aT = AB.alloc(16, NT)
        wbs = [AB.alloc(16, 512) for _ in range(2)]
        norm_stage(x, 0, aT)
        if upto == -3:
            return
        fst = [AFl.alloc(512) for _ in range(3)]
        bst = [AB.alloc(512) for _ in range(3)]
        fq = [AB.alloc(NT) for _ in range(2)]
        blk = 0
        for g, (win, dil) in enumerate(DIL):
            Ls = S // dil
            keep = min(win, S)
            for part in range(3):
                for hb in range(4):
                    n0 = g * 6144 + part * 2048 + hb * 512
                    if blk >= int(os.environ.get('KLIMIT', '999')):
                        continue
                    wb = wbs[blk % 2]
                    blk += 1
                    wload(w_in_a, 0, 16, n0, 512, wb)
                    if part < 2:
                        dst = QT0 if part == 0 else KT0
                        for j in range(4):
                            h = hb * 4 + j
                            stg = nxt("fq", fq)
                            for (t0, ntok) in TOKBLK:
                                ps = nxt("psA", psA)
                                mm_feat(ps, aT, t0, ntok, wb, 16, j)
                                if ntok == 512:
                                    m0 = t0 // dil
                                    if dil == 1:
                                        CP(stg[:, t0:t0 + 512], ps[:, :512])
                                    else:
                                        ov = stg[:, 0:S].rearrange("p (r m) -> p m r", r=dil)[:, m0:m0 + 512 // dil, :]
                                        iv = ps[:, :512].rearrange("p (m r) -> p m r", r=dil)
                                        CP(ov, iv)
                                else:
                                    CP(stg[:, S:S + 4], ps[:, :4])
                            row = (g * 16 + h) * 128
                            DMA(dst[row:row + 128, :], stg[:, :])
                    if part >= 1:
                        kv = part - 1
                        for (i, t0, nt) in TILES:
                            in_keep = (i == 16) or (t0 >= S - keep)
                            if part == 1 and not in_keep:
                                continue
                            if part == 2 and i < int(os.environ.get('KTMIN', '0')):
                                continue
                            ps = nxt("psA", psA)
                            mm_tok(ps, aT, t0, nt, wb, 16, 512)
                            sf = None
                            if in_keep:
                                sf = nxt("fst", fst)
                                CP(sf[:nt], ps[:nt, :])
                                if i == 16:
                                    DMA(o_ds[g][win - 4:win, kv, hb * 512:(hb + 1) * 512], sf[:nt])
                                else:
                                    r0 = t0 - (S - keep)
                                    DMA(o_dp[g][r0:r0 + nt, kv, hb * 512:(hb + 1) * 512], sf[:nt])
                            if part == 2:
                                sb = nxt("bst", bst)
                                CP(sb[:nt], sf[:nt] if sf is not None else ps[:nt, :])
                                DMA(V0[g, t0:t0 + nt, hb * 512:(hb + 1) * 512], sb[:nt])
        if upto == -4:
            return
        for g, (win, dil) in enumerate(DIL):
            stt_ = (st128, st512, st2048)[g]
            nrow = win - 4
            step = 512
            for r0 in range(0, nrow, step):
                n = min(step, nrow - r0)
                DMA(o_ds[g][r0:r0 + n, :, :], stt_[4 + r0:4 + r0 + n, :, :])

    l0_proj()
    if upto <= 1:
        return finish(nc, P, st)

    def softmax_pv(nq, ps_chunks, nk, bias_ap, vget, o_dst, nm_dst, l_dst, ss, pb, pts):
        for (pa, c0, n) in ps_chunks:
            STT(ss[:nq, c0:c0 + n], pa, SCALE, bias_ap[:, c0:c0 + n], ALU.mult, ALU.add)
        RED(nm_dst, ss[:nq, :nk], ALU.max, negate=True)
        ACT(pb[:nq, :nk], ss[:nq, :nk], AF.Exp, bias=nm_dst, scale=1.0)
        RED(l_dst, pb[:nq, :nk], ALU.add)
        cnt["rl"] = cnt.get("rl", 0) + 1
        rl = small[:nq, 64 + cnt["rl"] % 64:65 + cnt["rl"] % 64]
        RECIP(rl, l_dst)
        nch = (nk + 127) // 128
        cnt["pv"] = cnt.get("pv", 0) + 1
        ceng = "act" if cnt["pv"] % 2 else "dve"
        per = max(1, 1024 // nq)
        for j0 in range(0, nch, per):
            pt = nxt("psT", psT)
            for j in range(j0, min(nch, j0 + per)):
                kc = min(128, nk - j * 128)
                TR(pt[:kc, (j - j0) * nq:(j - j0 + 1) * nq], pb[:nq, j * 128:j * 128 + kc], idb[:nq, :nq])
            j1 = min(nch, j0 + per)
            nfull = sum(1 for j in range(j0, j1) if nk - j * 128 >= 128)
            if nfull:
                CP(pts[:, j0:j0 + nfull, :nq], pt[:, 0:nfull * nq].rearrange("p (j q) -> p j q", q=nq), eng=ceng)
            if nfull < j1 - j0:
                j = j0 + nfull
                kc = nk - j * 128
                CP(pts[:kc, j, :nq], pt[:kc, nfull * nq:(nfull + 1) * nq], eng=ceng)
        po = nxt("psO", psO)
        for j in range(nch):
            va, kc = vget(j)
            MM(po[:nq, :128], lhsT=pts[:kc, j, :nq], rhs=va, start=(j == 0), stop=(j == nch - 1))
        ACT(o_dst, po[:nq, :128], AF.Copy, scale=rl)

    def l0_attn():
        AB.reset()
        AFl.reset()
        QTb = [AB.alloc(16, 128) for _ in range(2)]
        KTb = [AB.alloc(16, 256) for _ in range(2)]
        Vb = [AB.alloc(2, 2048) for _ in range(2)]
        Pb = [AB.alloc(640) for _ in range(2)]
        PTs = [AB.alloc(5, 128) for _ in range(2)]
        bias_g = AFl.alloc(16, 256)
        Ssb = [AFl.alloc(640) for _ in range(2)]
        Ot = [AFl.alloc(2048) for _ in range(2)]
        mlt = [AFl.alloc(32) for _ in range(2)]
        for g, (win, dil) in enumerate(DIL):
            Ls = S // dil
            nb = Ls // 128
            for h in range(16):
                DMA(bias_g[:, h, :], tz_ap("l0p%d" % g, h, 127, 256))
            for r in range(dil):
                for b in range(nb):
                    col0 = r * Ls + b * 128
                    nprev = 1 if b > 0 else 0
                    nk = 128 * (1 + nprev)
                    kcol0 = col0 - 128 * nprev
                    qt = nxt("QTb", QTb)
                    kt = nxt("KTb", KTb)
                    vb = nxt("Vb", Vb)
                    for h4 in range(4):
                        rows = slice(g * 2048 + h4 * 512, g * 2048 + (h4 + 1) * 512)
                        DMA(qt[:, h4 * 4:(h4 + 1) * 4, :], QT0[rows, col0:col0 + 128].rearrange("(h d) c -> d h c", d=128))
                        DMA(kt[:, h4 * 4:(h4 + 1) * 4, :nk], KT0[rows, kcol0:kcol0 + nk].rearrange("(h d) c -> d h c", d=128))
                    for j in range(1 + nprev):
                        bb = b - nprev + j
                        row0 = bb * 128 * dil + r
                        DMA(vb[:, j, :], bass.AP(tensor=V0.tensor, offset=g * NT * D + row0 * D, ap=[[dil * D, 128], [1, D]]))
                    ot = nxt("Ot", Ot)
                    ml = nxt("mlt", mlt)
                    for h in range(16):
                        ps = nxt("psA", psA)
                        MM(ps[:, :nk], lhsT=qt[:, h, :], rhs=kt[:, h, :nk], start=True, stop=True)
                        softmax_pv(128, [(ps[:, :nk], 0, nk)], nk, bias_g[:, h, 256 - nk:256],
                                   lambda j, vb=vb, h=h: (vb[:, j, h * 128:(h + 1) * 128], 128),
                                   ot[:, h * 128:(h + 1) * 128], ml[:, h:h + 1], ml[:, 16 + h:17 + h],
                                   nxt("Ssb", Ssb), nxt("Pb", Pb), nxt("PTs", PTs))
                    rq0 = b * 128 * dil + r
                    DMA(bass.AP(tensor=OG.tensor, offset=g * NT * D + rq0 * D, ap=[[dil * D, 128], [1, D]]), ot)
                    DMA(bass.AP(tensor=ML.tensor, offset=g * NT * 32 + rq0 * 32, ap=[[dil * 32, 128], [1, 32]]), ml)

    def l0_attn_sample():
        AB.reset()
        AFl.reset()
        KTs = AB.alloc(16, 516)
        Vs = AB.alloc(5, 2048)
        qts = AB.alloc(16, 4)
        Pb = [AB.alloc(640) for _ in range(2)]
        PTs = [AB.alloc(5, 128) for _ in range(2)]
        kst = [AFl.alloc(2048) for _ in range(2)]
        bias_sb = [AFl.alloc(516) for _ in range(2)]
        Ssb = [AFl.alloc(640) for _ in range(2)]
        ot = AFl.alloc(2048)
        ml = AFl.alloc(32)
        for g, (win, dil) in enumerate(DIL):
            stt_ = (st128, st512, st2048)[g]
            ntile = 1 if g == 0 else 4
            nkey = ntile * 128
            nk = nkey + 4
            for ti in range(ntile):
                if g < 2:
                    off_r, pstep = ti * 128, 1
                else:
                    off_r, pstep = ti, 16
                for kv in range(2):
                    kb = nxt("kst", kst)
                    DMA(kb, bass.AP(tensor=stt_.tensor, offset=off_r * 2 * D + kv * D, ap=[[pstep * 2 * D, 128], [1, D]]))
                    if kv == 0:
                        for h in range(16):
                            pq = nxt("psO", psO)
                            TR(pq[:, :128], kb[:, h * 128:(h + 1) * 128], idf[:, :])
                            CP(KTs[:, h, ti * 128:(ti + 1) * 128], pq[:, :128])
                    else:
                        CP(Vs[:, ti, :], kb)
            DMA(KTs[:, :, nkey:nkey + 4], KT0[g * 2048:(g + 1) * 2048, S:S + 4].rearrange("(h d) c -> d h c", d=128))
            DMA(qts[:, :, :], QT0[g * 2048:(g + 1) * 2048, S:S + 4].rearrange("(h d) c -> d h c", d=128))
            DMA(Vs[0:4, ntile, :], V0[g, S:S + 4, :])
            for h in range(16):
                bias_s = nxt("bias_sb", bias_sb)
                DMA(bias_s[0:4, :nk], tb_ap("l0s%d" % g, h, 0, nk, 4, nk))
                chunks = []
                c0 = 0
                while c0 < nk:
                    n = min(512, nk - c0)
                    ps = nxt("psA", psA)
                    MM(ps[:4, :n], lhsT=qts[:, h, :], rhs=KTs[:, h, c0:c0 + n], start=True, stop=True)
                    chunks.append((ps[:4, :n], c0, n))
                    c0 += n
                softmax_pv(4, chunks, nk, bias_s[0:4, :nk],
                           lambda j, h=h, nk=nk: (Vs[:min(128, nk - j * 128), j, h * 128:(h + 1) * 128], min(128, nk - j * 128)),
                           ot[0:4, h * 128:(h + 1) * 128], ml[0:4, h:h + 1], ml[0:4, 16 + h:17 + h],
                           nxt("Ssb", Ssb), nxt("Pb", Pb), nxt("PTs", PTs))
            DMA(OG[g, S:S + 4, :], ot[0:4, :])
            DMA(ML[g, S:S + 4, :], ml[0:4, :])

    def out_proj(aT, w2d, Hsrc, Hdst, kch=16, lhs_loader=None):
        wbs = [AB.alloc(kch if kch <= 16 else 22, 512) for _ in range(2)]
        hr = [AFl.alloc(512) for _ in range(3)]
        for nb_ in range(4):
            wb = wbs[nb_ % 2]
            wload(w2d, 0, kch, nb_ * 512, 512, wb)
            for (i, t0, nt) in TILES:
                ps = nxt("psA", psA)
                mm_tok(ps, aT, t0, nt, wb, kch, 512)
                h_ = nxt("hr", hr)
                DMA(h_[:nt], Hsrc[t0:t0 + nt, nb_ * 512:(nb_ + 1) * 512])
                TT(h_[:nt], ps[:nt, :], h_[:nt], ALU.add)
                DMA(Hdst[t0:t0 + nt, nb_ * 512:(nb_ + 1) * 512], h_[:nt])

    def l0_combine_out():
        AB.reset()
        AFl.reset()
        aT = AB.alloc(16, NT)
        ogs = [AFl.alloc(2048) for _ in range(3)]
        acc = AFl.alloc(2048)
        mls = [AFl.alloc(32) for _ in range(3)]
        wk = AFl.alloc(8, 16)
        at = AB.alloc(2048)
        for (i, t0, nt) in TILES:
            for g in range(3):
                DMA(ogs[g][:nt], OG[g, t0:t0 + nt, :])
                DMA(mls[g][:nt], ML[g, t0:t0 + nt, :])
            mn = wk[:nt, 0, :]
            TT(mn, mls[0][:nt, 0:16], mls[1][:nt, 0:16], ALU.min)
            TT(mn, mn, mls[2][:nt, 0:16], ALU.min)
            den = wk[:nt, 1, :]
            for g in range(3):
                e = wk[:nt, 2 + g, :]
                TT(e, mn, mls[g][:nt, 0:16], ALU.subtract)
                ACT(e, e, AF.Exp)
                TT(e, e, mls[g][:nt, 16:32], ALU.mult)
                if g == 0:
                    CP(den, e, eng="dve")
                else:
                    TT(den, den, e, ALU.add)
            RECIP(den, den)
            for g in range(3):
                e = wk[:nt, 2 + g, :]
                TT(e, e, den, ALU.mult)
                ov = ogs[g][:nt].rearrange("p (h d) -> p h d", h=16)
                wv = e.unsqueeze(2).to_broadcast([nt, 16, 128])
                if g == 0:
                    TT(acc[:nt].rearrange("p (h d) -> p h d", h=16), ov, wv, ALU.mult)
                else:
                    TT(ov, ov, wv, ALU.mult)
                    TT(acc[:nt], acc[:nt], ogs[g][:nt], ALU.add)
            CP(at[:nt], acc[:nt], eng="act")
            for q in range(4):
                pt = nxt("psT", psT)
                for j in range(4):
                    c = q * 4 + j
                    TR(pt[:, j * 128:j * 128 + nt], at[:nt, c * 128:(c + 1) * 128], idb[:nt, :nt])
                CP(aT[:, q * 4:(q + 1) * 4, t0:t0 + nt],
                   pt[:, 0:512].rearrange("p (j t) -> p j t", j=4)[:, :, :nt])
        out_proj(aT, w_out_a, x, HA)

    l0_attn()
    l0_attn_sample()
    if upto <= 2:
        return finish(nc, P, st)
    l0_combine_out()
    if upto <= 3:
        return finish(nc, P, st)

    def load_cols_T(dst, src2d, nrow, tmp):
        for r in range(nrow):
            DMA(tmp[0:44, :], src2d[r, :].rearrange("(c p) -> c p", p=128))
            pq = nxt("psO", psO)
            TR(pq[:, :44], tmp[0:44, :], idf[0:44, 0:44])
            CP(dst[:, :, r], pq[:, :44])

    def ffn(li, Hsrc, Hdst):
        AB.reset()
        AFl.reset()
        aT = AB.alloc(16, NT)
        wbs = [AB.alloc(16, 512) for _ in range(2)]
        norm_stage(Hsrc, 1 + 3 * li, aT)
        AFl.reset()
        Gs = [AFl.alloc(2056) for _ in range(2)]
        Cs = [AFl.alloc(2056) for _ in range(2)]
        cwb = AFl.alloc(44, 4)
        tmp = AFl.alloc(128)
        uTst = [AB.alloc(NT) for _ in range(2)]
        load_cols_T(cwb[:, :, 0:3], conv_w[li], 3, tmp)
        load_cols_T(cwb[:, :, 3:4], conv_b[li:li + 1, :], 1, tmp)
        for G in Gs:
            MEMSET(G[:, 0:2], 0.0)
        for fb in range(11):
            wg, wv = wbs
            wload(w_ffn_in[li], 0, 16, fb * 512, 512, wg)
            wload(w_ffn_in[li], 0, 16, DFF + fb * 512, 512, wv)
            for j in range(4):
                c = fb * 4 + j
                G = nxt("Gs", Gs)
                C = nxt("Cs", Cs)
                for (t0, ntok) in TOKBLK:
                    ps = nxt("psA", psA)
                    mm_feat(ps, aT, t0, ntok, wg, 16, j)
                    if ntok == 512:
                        CP(G[:, 2 + t0:2 + t0 + 512], ps[:, :512])
                    else:
                        CP(G[:, 2052:2056], ps[:, :4])
                DMA(G[:, 2050:2052], stconv[li, :, c * 128:(c + 1) * 128].rearrange("t p -> p t"), allow_slow_non_contiguous=True)
                w0, w1, w2, bb = (cwb[:, c, k:k + 1] for k in range(4))
                for (o0, n, g0) in ((0, 2048, 0), (2048, 4, 2050)):
                    ACT(C[:, o0:o0 + n], G[:, g0 + 2:g0 + 2 + n], AF.Identity, bias=bb, scale=w2)
                    STT(C[:, o0:o0 + n], G[:, g0 + 1:g0 + 1 + n], w1, C[:, o0:o0 + n], ALU.mult, ALU.add)
                    STT(C[:, o0:o0 + n], G[:, g0:g0 + n], w0, C[:, o0:o0 + n], ALU.mult, ALU.add)
                ACT(C[:, :NT], C[:, :NT], AF.Gelu_apprx_tanh)
                DMA(o_convp[li, :, c * 128:(c + 1) * 128].rearrange("t p -> p t"), G[:, 2048:2050], allow_slow_non_contiguous=True)
                DMA(o_convs[li, :, c * 128:(c + 1) * 128].rearrange("t p -> p t"), G[:, 2054:2056], allow_slow_non_contiguous=True)
                us = nxt("uTst", uTst)
                for (t0, ntok) in TOKBLK:
                    ps = nxt("psA", psA)
                    mm_feat(ps, aT, t0, ntok, wv, 16, j)
                    TT(us[:, t0:t0 + ntok], ps[:, :ntok], C[:, t0:t0 + ntok], ALU.mult)
                DMA(UT[c * 128:(c + 1) * 128, :], us)
        AB.reset()
        AFl.reset()
        wbs2 = [AB.alloc(44, 512) for _ in range(2)]
        uts = [AB.alloc(44, 128) for _ in range(2)]
        hr = [AFl.alloc(512) for _ in range(3)]
        for nb_ in range(4):
            wb = wbs2[nb_ % 2]
            wload(w_ffn_out[li], 0, 44, nb_ * 512, 512, wb)
            for (i, t0, nt) in TILES:
                ut = nxt("uts", uts)
                for q in range(4):
                    DMA(ut[:, q * 11:(q + 1) * 11, :nt],
                        UT[q * 11 * 128:(q + 1) * 11 * 128, t0:t0 + nt].rearrange("(c p) t -> p c t", p=128))
                ps = nxt("psA", psA)
                for k in range(44):
                    MM(ps[:nt, :512], lhsT=ut[:, k, :nt], rhs=wb[:, k, :512], start=(k == 0), stop=(k == 43))
                h_ = nxt("hr", hr)
                DMA(h_[:nt], Hsrc[t0:t0 + nt, nb_ * 512:(nb_ + 1) * 512])
                TT(h_[:nt], ps[:nt, :], h_[:nt], ALU.add)
                DMA(Hdst[t0:t0 + nt, nb_ * 512:(nb_ + 1) * 512], h_[:nt])

    def ple(li, Hsrc, Hdst):
        AB.reset()
        AFl.reset()
        aT = AB.alloc(16, NT)
        pT = AB.alloc(2, NT)
        wbs = [AB.alloc(16, 512) for _ in range(2)]
        wps = [AB.alloc(2, 512) for _ in range(2)]
        norm_stage(Hsrc, 2 + 3 * li, aT, p_layer=li, pT=pT)
        AFl.reset()
        hr = [AFl.alloc(512) for _ in range(3)]
        gs = [AFl.alloc(512) for _ in range(3)]
        for nb_ in range(4):
            wb = wbs[nb_ % 2]
            wp = wps[nb_ % 2]
            wload(w_ple_gate[li], 0, 16, nb_ * 512, 512, wb)
            wload(w_ple_proj[li], 0, 2, nb_ * 512, 512, wp)
            for (i, t0, nt) in TILES:
                ps = nxt("psA", psA)
                mm_tok(ps, aT, t0, nt, wb, 16, 512)
                g_ = nxt("gs", gs)
                ACT(g_[:nt], ps[:nt, :], AF.Sigmoid)
                if not os.environ.get("KPLE"):
                    ps2 = nxt("psA", psA)
                    mm_tok(ps2, pT, t0, nt, wp, 2, 512)
                    TT(g_[:nt], ps2[:nt, :], g_[:nt], ALU.mult)
                h_ = nxt("hr", hr)
                DMA(h_[:nt], Hsrc[t0:t0 + nt, nb_ * 512:(nb_ + 1) * 512])
                TT(h_[:nt], h_[:nt], g_[:nt], ALU.add)
                DMA(Hdst[t0:t0 + nt, nb_ * 512:(nb_ + 1) * 512], h_[:nt])

    ffn(0, HA, HB)
    if upto <= 4:
        return finish(nc, P, st)
    ple(0, HB, HA)
    if upto <= 5:
        return finish(nc, P, st)

    def l1_proj():
        AB.reset()
        AFl.reset()
        aT = AB.alloc(16, NT)
        wbs = [AB.alloc(16, 512) for _ in range(2)]
        norm_stage(HA, 3, aT)
        AFl.reset()
        fst = [AFl.alloc(512) for _ in range(3)]
        bst = [AB.alloc(512) for _ in range(3)]
        fq = [AB.alloc(NT) for _ in range(2)]
        for blk in range(11):
            wb = wbs[blk % 2]
            ncols = 512 if blk < 10 else 48
            wload(w_in_b, 0, 16, blk * 512, ncols, wb)
            feat_dst = None
            if blk < 4:
                feat_dst = (QT1, blk * 512)
            elif blk in (4, 5):
                feat_dst = (XT1, (blk - 4) * 512)
            elif blk == 6:
                feat_dst = (KST, 0)
            elif blk == 8:
                feat_dst = (KWT, 0)
            if feat_dst is not None:
                dst, r0 = feat_dst
                for j in range(4):
                    stg = nxt("fq", fq)
                    for (t0, ntok) in TOKBLK:
                        ps = nxt("psA", psA)
                        mm_feat(ps, aT, t0, ntok, wb, 16, j)
                        CP(stg[:, t0:t0 + ntok], ps[:, :ntok])
                    DMA(dst[r0 + j * 128:r0 + (j + 1) * 128, :], stg[:, :])
            if blk >= 4:
                for (i, t0, nt) in TILES:
                    need_f32 = (4 <= blk <= 7) or (blk in (8, 9) and (i == 16 or t0 >= S - 512)) or blk == 10
                    need_b16 = blk in (7, 9)
                    if not (need_f32 or need_b16):
                        continue
                    ps = nxt("psA", psA)
                    mm_tok(ps, aT, t0, nt, wb, 16, ncols)
                    sf = None
                    if need_f32:
                        sf = nxt("fst", fst)
                        if blk == 10:
                            ACT(sf[:nt, :48], ps[:nt, :48], AF.Sigmoid)
                            DMA(G1[t0:t0 + nt, :], sf[:nt, :48])
                        else:
                            CP(sf[:nt], ps[:nt, :])
                            if blk <= 7:
                                kvi = blk - 4
                                if i == 16:
                                    DMA(o_kvs[:, kvi * 512:(kvi + 1) * 512], sf[:nt])
                                else:
                                    DMA(o_kvp[t0:t0 + nt, kvi * 512:(kvi + 1) * 512], sf[:nt])
                            else:
                                kv = blk - 8
                                if i == 16:
                                    DMA(o_wins[508:512, kv, :], sf[:nt])
                                else:
                                    DMA(o_winp[t0 - (S - 512):t0 - (S - 512) + nt, kv, :], sf[:nt])
                    if need_b16:
                        sb = nxt("bst", bst)
                        CP(sb[:nt], sf[:nt] if sf is not None else ps[:nt, :])
                        DMA((VS1 if blk == 7 else VW1)[t0:t0 + nt, :], sb[:nt])
        DMA(o_wins[0:508, :, :], stwin[4:512, :, :])

    l1_proj()
    if upto <= 6:
        return finish(nc, P, st)

    def sub_ap(v, extra, fstep, fcount):
        return bass.AP(tensor=v.tensor, offset=v.offset + extra, ap=[list(v.ap[0]), [fstep, fcount]])

    def compress_setup(kvi, w1b, w2b, peTb, cols, tmpf):
        for r0 in range(0, 32, 4):
            DMA(w1b[:, r0:r0 + 4, :], cmp_w1[kvi, r0:r0 + 4, :, :].rearrange("r d f -> d r f"), eng="pool")
        DMA(w2b[:, :], cmp_w2[kvi, :, :], eng="pool")
        DMA(tmpf[0:32, :], cmp_pe[kvi, :, :])
        pq = nxt("psO", psO)
        TR(pq[:, :32], tmpf[0:32, :], idf[0:32, 0:32])
        CP(peTb[:, :], pq[:, :32])
        DMA(cols[:, 2:3], cmp_b1[kvi:kvi + 1, :].rearrange("o f -> f o"), allow_slow_non_contiguous=True)
        DMA(cols[:, 1:2], cmp_b2[kvi:kvi + 1, :].rearrange("o f -> f o"), allow_slow_non_contiguous=True)
        pq = nxt("psO", psO)
        for r in range(32):
            MM(pq[:, 0:1], lhsT=w1b[:, r, :], rhs=peTb[:, r:r + 1], start=(r == 0), stop=(r == 31))
        TT(cols[:, 0:1], pq[:, 0:1], cols[:, 2:3], ALU.add)

    def nsa_prompt():
        AB.reset()
        AFl.reset()
        KsT = AB.alloc(4, S)
        KwT = AB.alloc(4, S)
        Vsel = AB.alloc(16, 512)
        Vwin = AB.alloc(16, 512)
        kcT = AB.alloc(4, 128)
        vc = AB.alloc(4, 128)
        w1b = AB.alloc(32, 128)
        w2b = AB.alloc(128)
        peTb = AB.alloc(32)
        XTt = AB.alloc(S)
        hg = AB.alloc(128)
        QTq = [AB.alloc(16, 128) for _ in range(2)]
        Pb = [AB.alloc(2048) for _ in range(2)]
        PTs = [AB.alloc(16, 128) for _ in range(2)]
        obf = AB.alloc(2048)
        Ssb = [AFl.alloc(2048) for _ in range(2)]
        biasb = [AFl.alloc(2048) for _ in range(2)]
        oacc = AFl.alloc(2048)
        pcp = AFl.alloc(4, 132)
        cols = AFl.alloc(8)
        tmpf = AFl.alloc(128)
        b2row = AFl.alloc(128)
        gts = AFl.alloc(48)
        sA = AFl.alloc(32)
        sB = AFl.alloc(32)
        wk = AFl.alloc(8, 32)
        tmpO = AFl.alloc(128)
        for kvh in range(4):
            DMA(KsT[:, kvh, :], KST[kvh * 128:(kvh + 1) * 128, 0:S])
            DMA(KwT[:, kvh, :], KWT[kvh * 128:(kvh + 1) * 128, 0:S])
        for q in range(4):
            DMA(Vsel[:, q * 4:(q + 1) * 4, :], VS1[q * 512:(q + 1) * 512, :].rearrange("(b p) c -> p b c", p=128))
            DMA(Vwin[:, q * 4:(q + 1) * 4, :], VW1[q * 512:(q + 1) * 512, :].rearrange("(b p) c -> p b c", p=128))
        for kvi in range(2):
            compress_setup(kvi, w1b, w2b, peTb, cols, tmpf)
            if kvi == 1:
                DMA(b2row[:, :], bass.AP(tensor=cmp_b2.tensor, offset=128, ap=[[0, 128], [1, 128]]))
            for kvh in range(4):
                DMA(XTt[:, :], XT1[(kvi * 4 + kvh) * 128:(kvi * 4 + kvh + 1) * 128, 0:S])
                ps = nxt("psA", psA)
                for r in range(32):
                    MM(ps[:, :127], lhsT=w1b[:, r, :], rhs=sub_ap(XTt, r, 16, 127), start=(r == 0), stop=(r == 31))
                ACT(hg[:, :127], ps[:, :127], AF.Gelu_apprx_tanh, bias=cols[:, 0:1], scale=1.0)
                pq = nxt("psO", psO)
                if kvi == 0:
                    MM(pq[:, :127], lhsT=w2b[:, :], rhs=hg[:, :127], start=True, stop=True)
                    ACT(kcT[:, kvh, :127], pq[:, :127], AF.Identity, bias=cols[:, 1:2], scale=1.0)
                else:
                    MM(pq[:127, :128], lhsT=hg[:, :127], rhs=w2b[:, :], start=True, stop=True)
                    TT(vc[:127, kvh, :], pq[:127, :128], b2row[:127, :], ALU.add)
        MEMSET(pcp[:, :, :], 0.0)
        for b in range(16):
            t0 = b * 128
            qt = nxt("QTq", QTq)
            for h4 in range(4):
                DMA(qt[:, h4 * 4:(h4 + 1) * 4, :], QT1[h4 * 512:(h4 + 1) * 512, t0:t0 + 128].rearrange("(h d) c -> d h c", d=128))
            DMA(gts[:, :], G1[t0:t0 + 128, :])
            DMA(sA[:, :], c_selA[t0:t0 + 128, :])
            DMA(sB[:, :], c_selB[t0:t0 + 128, :])
            DMA(cols[:, 4:5], c_rowv[t0:t0 + 128, :])
            nk_s = 128 * (b + 1)
            kb0 = max(0, b - 4)
            nk_w = 128 * (b - kb0 + 1)
            for kvh in range(4):
                for g in range(4):
                    h = kvh * 4 + g
                    ps = nxt("psA", psA)
                    MM(ps[:, :127], lhsT=qt[:, h, :], rhs=kcT[:, kvh, :127], start=True, stop=True)
                    bb_ = nxt("biasb", biasb)
                    DMA(bb_[:, :127], tb_ap("cmpp", h, 120 - 8 * b, 247, 128, 127))
                    ss = nxt("Ssb", Ssb)
                    STT(ss[:, :127], ps[:, :127], SCALE, bb_[:, :127], ALU.mult, ALU.add)
                    nm = wk[:, 0, 0:1]
                    l_ = wk[:, 0, 1:2]
                    rl = wk[:, 0, 2:3]
                    RED(nm, ss[:, :127], ALU.max, negate=True)
                    ACT(ss[:, 128:255], ss[:, :127], AF.Exp, bias=nm, scale=1.0)
                    RED(l_, ss[:, 128:255], ALU.add)
                    TS(l_, l_, 1e-30, ALU.max)
                    RECIP(rl, l_)
                    TT(rl, rl, cols[:, 4:5], ALU.mult)
                    TS(ss[:, 128:255], ss[:, 128:255], rl, ALU.mult)
                    if g == 0:
                        CP(pcp[:, kvh, 1:128], ss[:, 128:255], eng="dve")
                    else:
                        TT(pcp[:, kvh, 1:128], pcp[:, kvh, 1:128], ss[:, 128:255], ALU.add)
                    pb = nxt("Pb", Pb)
                    CP(pb[:, :127], ss[:, 128:255], eng="act")
                    pt = nxt("psT", psT)
                    TR(pt[:127, :128], pb[:, :127], idb[:, :])
                    pts = nxt("PTs", PTs)
                    CP(pts[:127, 0, :], pt[:127, :128])
                    po = nxt("psO", psO)
                    MM(po[:, :128], lhsT=pts[:127, 0, :], rhs=vc[:127, kvh, :], start=True, stop=True)
                    ACT(oacc[:, h * 128:(h + 1) * 128], po[:, :128], AF.Copy, scale=gts[:, h:h + 1])
                pv = pcp[:, kvh, :]
                va = pv[:, 0:128].rearrange("p (j f) -> p j f", f=4)
                vb_ = pv[:, 4:132].rearrange("p (j f) -> p j f", f=4)
                t1 = wk[:, 1, :]
                psl = wk[:, 2, :]
                sc = wk[:, 3, :]
                sc2 = wk[:, 4, :]
                m8 = wk[:, 5, 0:8]
                m8b = wk[:, 5, 8:16]
                mneg = wk[:, 6, :]
                TT(t1, va[:, :, 1], va[:, :, 2], ALU.add)
                TT(t1, t1, va[:, :, 3], ALU.add)
                STT(psl, t1, 2.0, va[:, :, 0], ALU.mult, ALU.add)
                TT(psl, psl, vb_[:, :, 0], ALU.add)
                TT(sc, psl, sA[:, :], ALU.mult)
                TT(sc, sc, sB[:, :], ALU.add)
                P.op("dve", lambda e, m8=m8, sc=sc: e.max(out=m8, in_=sc), reads=[sc], writes=[m8])
                P.op("dve", lambda e, m8=m8, sc=sc, sc2=sc2: e.match_replace(out=sc2, in_to_replace=m8, in_values=sc, imm_value=-1e9),
                     reads=[sc, m8], writes=[sc2])
                P.op("dve", lambda e, m8b=m8b, sc2=sc2: e.max(out=m8b, in_=sc2), reads=[sc2], writes=[m8b])
                TS(mneg, sc, m8b[:, 7:8], ALU.is_ge)
                TS(mneg, mneg, -1.0, ALU.add, -NEG, ALU.mult)
                for g in range(4):
                    h = kvh * 4 + g
                    for br in (1, 2):
                        if br == 1:
                            nk, kc0, KT_, V_, vb0 = nk_s, 0, KsT, Vsel, 0
                        else:
                            nk, kc0, KT_, V_, vb0 = nk_w, kb0 * 128, KwT, Vwin, kb0
                        chunks = []
                        c0 = 0
                        while c0 < nk:
                            n = min(512, nk - c0)
                            ps = nxt("psA", psA)
                            MM(ps[:, :n], lhsT=qt[:, h, :], rhs=KT_[:, kvh, kc0 + c0:kc0 + c0 + n], start=True, stop=True)
                            chunks.append((ps[:, :n], c0, n))
                            c0 += n
                        bb_ = nxt("biasb", biasb)
                        if br == 1:
                            DMA(bb_[:, :nk], tz_ap("selp", h, 127 + 1920 - 128 * b, nk))
                            TT(bb_[:, :nk].rearrange("p (j s) -> p j s", s=64), bb_[:, :nk].rearrange("p (j s) -> p j s", s=64),
                               mneg[:, 0:nk // 64].unsqueeze(2).to_broadcast([128, nk // 64, 64]), ALU.add)
                        else:
                            DMA(bb_[:, :nk], tz_ap("winp", h, 127 + 640 - nk, nk))
                        softmax_pv(128, chunks, nk, bb_[:, :nk],
                                   lambda j, V_=V_, vb0=vb0, kvh=kvh: (V_[:, vb0 + j, kvh * 128:(kvh + 1) * 128], 128),
                                   tmpO[:, :], wk[:, 7, 0:1], wk[:, 7, 1:2],
                                   nxt("Ssb", Ssb), nxt("Pb", Pb), nxt("PTs", PTs))
                        STT(oacc[:, h * 128:(h + 1) * 128], tmpO[:, :], gts[:, br * 16 + h:br * 16 + h + 1],
                            oacc[:, h * 128:(h + 1) * 128], ALU.mult, ALU.add)
            CP(obf[:, :], oacc[:, :], eng="act")
            DMA(OM[t0:t0 + 128, :], obf[:, :])

    def mixer_out(w2d, Hsrc, Hdst):
        AB.reset()
        AFl.reset()
        aT = AB.alloc(16, NT)
        ats = [AB.alloc(2048) for _ in range(2)]
        for (i, t0, nt) in TILES:
            at = ats[i % 2]
            DMA(at[:nt], OM[t0:t0 + nt, :])
            for q in range(4):
                pt = nxt("psT", psT)
                for j in range(4):
                    c = q * 4 + j
                    TR(pt[:, j * 128:j * 128 + nt], at[:nt, c * 128:(c + 1) * 128], idb[:nt, :nt])
                CP(aT[:, q * 4:(q + 1) * 4, t0:t0 + nt],
                   pt[:, 0:512].rearrange("p (j t) -> p j t", j=4)[:, :, :nt])
        out_proj(aT, w2d, Hsrc, Hdst)

    def final_norm(Hsrc):
        AB.reset()
        AFl.reset()
        xts = [AFl.alloc(2048) for _ in range(2)]
        gb = AFl.alloc(2048)
        junk = AFl.alloc(2048)
        ys = [AFl.alloc(2048) for _ in range(2)]
        DMA(gb, bass.AP(tensor=norms.tensor, offset=6 * D, ap=[[0, 128], [1, D]]))
        for (i, t0, nt) in TILES:
            xt = xts[i % 2]
            yt = ys[i % 2]
            ss = small[:, 16 + (i % 2) * 4:16 + (i % 2) * 4 + 1]
            rs = small[:, 16 + (i % 2) * 4 + 1:16 + (i % 2) * 4 + 2]
            DMA(xt[:nt], Hsrc[t0:t0 + nt, :])
            P.op("act", lambda e, xt=xt, ss=ss, nt=nt: e.activation(out=junk[:nt], in_=xt[:nt], func=AF.Square, accum_out=ss[:nt]),
                 reads=[xt[:nt]], writes=[junk[:nt], ss[:nt]])
            TS(rs[:nt], ss[:nt], 1.0 / D, ALU.mult, EPS, ALU.add)
            ACT(rs[:nt], rs[:nt], AF.Sqrt)
            RECIP(rs[:nt], rs[:nt])
            STT(yt[:nt], xt[:nt], rs[:nt], gb[:nt], ALU.mult, ALU.mult)
            DMA(o_y[t0:t0 + nt, :], yt[:nt])

    def nsa_sample():
        AB.reset()
        AFl.reset()
        pgs = [AFl.alloc(2048) for _ in range(2)]
        ptf = AFl.alloc(128)
        stg = [AB.alloc(12, 512) for _ in range(2)]
        vst = [AB.alloc(512) for _ in range(2)]
        DMA(pt_i32[:, :], bass.AP(tensor=ptab.tensor, offset=0, ap=[[0, 128], [1, 128]]))
        DMA(iota_sb[:, :], c_iota[:, :])
        DMA(selm_sb[:, :], c_selm[:, :])
        DMA(selmT_sb[:, :], c_selm.rearrange("a b -> b a"), allow_slow_non_contiguous=True)
        CP(ptf[:, :], pt_i32[:, :], eng="dve")
        TS(ptf[:, :], ptf[:, :], 128.0, ALU.mult, iota_sb[:, 0:1], ALU.add)
        CP(idx_i32[:, :], ptf[:, :], eng="dve")
        for n in range(NPAGE):
            pg = pgs[n % 2]
            sg = stg[(n // 4) % 2]
            P.dma_custom("pool", lambda e, pg=pg, n=n: e.indirect_dma_start(
                out=pg[:, :], out_offset=None, in_=cache[:, :],
                in_offset=bass.IndirectOffsetOnAxis(ap=idx_i32[:, n:n + 1], axis=0)),
                reads=[cache[:, :], idx_i32[:, n:n + 1]], writes=[pg[:, :]])
            for grp in range(3):
                pq = nxt("psA", psA)
                for kvh in range(4):
                    c0 = (grp * 4 + kvh) * 128
                    TR(pq[:, kvh * 128:(kvh + 1) * 128], pg[:, c0:c0 + 128], idf[:, :])
                CP(sg[:, grp * 4:(grp + 1) * 4, (n % 4) * 128:(n % 4 + 1) * 128], pq[:, :].rearrange("p (k r) -> p k r", k=4))
            vs_ = vst[n % 2]
            CP(vs_[:, :], pg[:, 1536:2048])
            DMA(CVS[n * 128:(n + 1) * 128, :], vs_[:, :])
            if n % 4 == 3:
                cc = (n // 4) * 512
                DMA(CXT[:, cc:cc + 512].rearrange("(k d) c -> d k c", d=128), sg[:, 0:8, :])
                DMA(CKST[:, cc:cc + 512].rearrange("(k d) c -> d k c", d=128), sg[:, 8:12, :])
        AB.reset()
        AFl.reset()
        kcT = AB.alloc(4, 1024)
        vc = AB.alloc(8, 4, 128)
        p0 = AB.p
        XTs = AB.alloc(PAST)
        w1b = AB.alloc(32, 128)
        w2b = AB.alloc(128)
        peTb = AB.alloc(32)
        hg = AB.alloc(1024)
        cols = AFl.alloc(8)
        tmpf = AFl.alloc(128)
        b2row = AFl.alloc(128)
        for kvi in range(2):
            compress_setup(kvi, w1b, w2b, peTb, cols, tmpf)
            if kvi == 1:
                DMA(b2row[:, :], bass.AP(tensor=cmp_b2.tensor, offset=128, ap=[[0, 128], [1, 128]]))
            for kvh in range(4):
                r0 = (kvi * 4 + kvh) * 128
                for q in range(4):
                    DMA(XTs[:, q * 4096:(q + 1) * 4096], CXT[r0:r0 + 128, q * 4096:(q + 1) * 4096])
                for (c0, n) in ((0, 512), (512, 511)):
                    ps = nxt("psA", psA)
                    for r in range(32):
                        MM(ps[:, :n], lhsT=w1b[:, r, :], rhs=sub_ap(XTs, r + 16 * c0, 16, n), start=(r == 0), stop=(r == 31))
                    ACT(hg[:, c0:c0 + n], ps[:, :n], AF.Gelu_apprx_tanh, bias=cols[:, 0:1], scale=1.0)
                if kvi == 0:
                    for (c0, n) in ((0, 512), (512, 511)):
                        pq = nxt("psO", psO)
                        MM(pq[:, :n], lhsT=w2b[:, :], rhs=hg[:, c0:c0 + n], start=True, stop=True)
                        ACT(kcT[:, kvh, c0:c0 + n], pq[:, :n], AF.Identity, bias=cols[:, 1:2], scale=1.0)
                else:
                    for ct in range(8):
                        kc = min(128, 1023 - ct * 128)
                        pq = nxt("psO", psO)
                        MM(pq[:kc, :128], lhsT=hg[:, ct * 128:ct * 128 + kc], rhs=w2b[:, :], start=True, stop=True)
                        TT(vc[:kc, ct, kvh, :], pq[:kc, :128], b2row[:kc, :], ALU.add)
        AB.p = p0
        AFl.reset()
        KsT = AB.alloc(16400)
        Vsel = AB.alloc(128, 128)
        vnew = AB.alloc(2, 128)
        qts = AB.alloc(16)
        pbh = AB.alloc(8704)
        ptsh = AB.alloc(68, 16)
        kwT = AB.alloc(640)
        vw = AB.alloc(4, 128)
        Pb2 = AB.alloc(640)
        PTs2 = AB.alloc(5, 128)
        o16b = AB.alloc(128)
        ssh = AFl.alloc(8208)
        bch = [AFl.alloc(516)]
        pcp = AFl.alloc(1040)
        wkA = AFl.alloc(264)
        t1b = AFl.alloc(264)
        pslb = AFl.alloc(264)
        scb = AFl.alloc(264)
        sc2b = AFl.alloc(264)
        mnegb = AFl.alloc(264)
        mneg16b = AFl.alloc(264)
        o16 = AFl.alloc(128)
        ohs = [AFl.alloc(128) for _ in range(2)]
        sAs = AFl.alloc(264)
        sBs = AFl.alloc(264)
        kwf = ssh[:, 5120:5632].rearrange("p (a b) -> p a b", a=4)
        Ssb2 = ssh[:, 6144:6784]
        tmpO = ssh[:, 7168:7296]
        DMA(sAs[0:4, :], c_selAs[:, :])
        DMA(sBs[0:4, :], c_selBs[:, :])
        MEMSET(pcp[0:4, :], 0.0)
        MEMSET(pslb[0:4, :], 0.0)
        for kvh in range(4):
            for q in range(4):
                DMA(KsT[:, q * 4096:(q + 1) * 4096], CKST[kvh * 128:(kvh + 1) * 128, q * 4096:(q + 1) * 4096])
            DMA(KsT[:, PAST:PAST + 4], KST[kvh * 128:(kvh + 1) * 128, S:S + 4])
            for q in range(8):
                DMA(Vsel[:, q * 16:(q + 1) * 16, :], CVS[q * 2048:(q + 1) * 2048, kvh * 128:(kvh + 1) * 128].rearrange("(n p) c -> p n c", p=128))
            DMA(vnew[0:4, 0, :], VS1[S:S + 4, kvh * 128:(kvh + 1) * 128])
            DMA(vnew[0:4, 1, :], VW1[S:S + 4, kvh * 128:(kvh + 1) * 128])
            DMA(qts[:, :].rearrange("d (g t) -> d g t", g=4), QT1[kvh * 512:(kvh + 1) * 512, S:S + 4].rearrange("(g d) t -> d g t", d=128))
            gt16 = wkA[0:16, 0:3]
            for g in range(4):
                for br in range(3):
                    col = br * 16 + kvh * 4 + g
                    DMA(wkA[4 * g:4 * g + 4, br:br + 1], G1[S:S + 4, col:col + 1], allow_slow_non_contiguous=True)
            ssc = ssh[0:16, 0:1024]
            pf = ssh[0:16, 1024:2048]
            bsm = ssh[0:16, 2048:3072]
            for g in range(4):
                DMA(ssh[4 * g:4 * g + 4, 2048:2048 + 1023], tb_ap("cmps", kvh * 4 + g, 0, 1024, 4, 1023))
            for (c0, n) in ((0, 512), (512, 511)):
                ps = nxt("psA", psA)
                MM(ps[:16, :n], lhsT=qts[:, :], rhs=kcT[:, kvh, c0:c0 + n], start=True, stop=True)
                STT(ssc[:, c0:c0 + n], ps[:16, :n], SCALE, bsm[:, c0:c0 + n], ALU.mult, ALU.add)
            nm = wkA[0:16, 4:5]
            l_ = wkA[0:16, 5:6]
            rl = wkA[0:16, 6:7]
            RED(nm, ssc[:, :1023], ALU.max, negate=True)
            ACT(pf[:, :1023], ssc[:, :1023], AF.Exp, bias=nm, scale=1.0)
            RED(l_, pf[:, :1023], ALU.add)
            RECIP(rl, l_)
            TS(pf[:, :1023], pf[:, :1023], rl, ALU.mult)
            for (c0, n) in ((0, 512), (512, 511)):
                pq = nxt("psO", psO)
                MM(pq[:4, :n], lhsT=selm_sb[:, :], rhs=pf[:, c0:c0 + n], start=True, stop=True)
                CP(pcp[0:4, 1 + c0:1 + c0 + n], pq[:4, :n], eng="dve")
            CP(pbh[0:16, :1023], pf[:, :1023], eng="act")
            pt = nxt("psT", psT)
            for j in range(8):
                kc = min(128, 1023 - j * 128)
                TR(pt[:kc, j * 16:(j + 1) * 16], pbh[0:16, j * 128:j * 128 + kc], idb[0:16, 0:16])
            CP(ptsh[:, 0:7, :], pt[:, 0:112].rearrange("p (j q) -> p j q", q=16), eng="dve")
            CP(ptsh[:127, 7, :], pt[:127, 112:128], eng="dve")
            po = nxt("psO", psO)
            for j in range(8):
                kc = min(128, 1023 - j * 128)
                MM(po[:16, :128], lhsT=ptsh[:kc, j, :], rhs=vc[:kc, j, kvh, :], start=(j == 0), stop=(j == 7))
            ACT(o16[0:16, :], po[:16, :128], AF.Copy, scale=gt16[:, 0:1])
            va = pcp[0:4, 0:1028].rearrange("p (j f) -> p j f", f=4)
            vb_ = pcp[0:4, 4:1032].rearrange("p (j f) -> p j f", f=4)
            t1 = t1b[0:4, 0:257]
            psl = pslb[0:4, 0:257]
            sc = scb[0:4, :]
            sc2 = sc2b[0:4, :]
            m8 = wkA[0:4, 8:16]
            m8b = wkA[0:4, 16:24]
            mneg = mnegb[0:4, :]
            mneg16 = mneg16b[0:16, :]
            TT(t1, va[:, :, 1], va[:, :, 2], ALU.add)
            TT(t1, t1, va[:, :, 3], ALU.add)
            STT(psl, t1, 2.0, va[:, :, 0], ALU.mult, ALU.add)
            TT(psl, psl, vb_[:, :, 0], ALU.add)
            TT(sc, pslb[0:4, :], sAs[0:4, :], ALU.mult)
            TT(sc, sc, sBs[0:4, :], ALU.add)
            P.op("dve", lambda e, m8=m8, sc=sc: e.max(out=m8, in_=sc), reads=[sc], writes=[m8])
            P.op("dve", lambda e, m8=m8, sc=sc, sc2=sc2: e.match_replace(out=sc2, in_to_replace=m8, in_values=sc, imm_value=-1e9),
                 reads=[sc, m8], writes=[sc2])
            P.op("dve", lambda e, m8b=m8b, sc2=sc2: e.max(out=m8b, in_=sc2), reads=[sc2], writes=[m8b])
            TS(mneg, sc, m8b[:, 7:8], ALU.is_ge)
            TS(mneg, mneg, -1.0, ALU.add, -NEG, ALU.mult)
            pq = nxt("psO", psO)
            MM(pq[:16, :264], lhsT=selmT_sb[:, :], rhs=mneg, start=True, stop=True)
            CP(mneg16, pq[:16, :264], eng="dve")
            for half in range(2):
                cl = list(range(0, 16)) if half == 0 else list(range(16, 33))
                for lc, c in enumerate(cl):
                    n = 512 if c < 32 else 4
                    ps = nxt("psA", psA)
                    MM(ps[:16, :n], lhsT=qts[:, :], rhs=KsT[:, c * 512:c * 512 + n], start=True, stop=True)
                    bc_ = nxt("bch", bch)
                    for g in range(4):
                        DMA(bc_[4 * g:4 * g + 4, :n], tb_ap("sels", kvh * 4 + g, c * 512, 16512, 4, n))
                    STT(ssh[0:16, lc * 512:lc * 512 + n], ps[:16, :n], SCALE, bc_[0:16, :n], ALU.mult, ALU.add)
                    if n == 512:
                        v3 = ssh[0:16, lc * 512:(lc + 1) * 512].rearrange("p (j s) -> p j s", s=64)
                        TT(v3, v3, mneg16[:, c * 8:c * 8 + 8].unsqueeze(2).to_broadcast([16, 8, 64]), ALU.add)
                    else:
                        v3 = ssh[0:16, lc * 512:lc * 512 + 4]
                        TT(v3, v3, mneg16[:, 256:257].to_broadcast([16, 4]), ALU.add)
                nkh = 16 * 512 if half == 0 else 16 * 512 + 4
                nmh = wkA[0:16, 24 + half:25 + half]
                lh = wkA[0:16, 26 + half:27 + half]
                RED(nmh, ssh[0:16, :nkh], ALU.max, negate=True)
                ACT(pbh[0:16, :nkh], ssh[0:16, :nkh], AF.Exp, bias=nmh, scale=1.0)
                RED(lh, pbh[0:16, :nkh], ALU.add)
                nch = (nkh + 127) // 128
                for j0 in range(0, nch, 64):
                    j1 = min(nch, j0 + 64)
                    pt = nxt("psT", psT)
                    for j in range(j0, j1):
                        kc = min(128, nkh - j * 128)
                        TR(pt[:kc, (j - j0) * 16:(j - j0 + 1) * 16], pbh[0:16, j * 128:j * 128 + kc], idb[0:16, 0:16])
                    nfull = sum(1 for j in range(j0, j1) if nkh - j * 128 >= 128)
                    if nfull:
                        CP(ptsh[:, j0:j0 + nfull, :], pt[:, 0:nfull * 16].rearrange("p (j q) -> p j q", q=16), eng="dve")
                    if nfull < j1 - j0:
                        kc = nkh - (j0 + nfull) * 128
                        CP(ptsh[:kc, j0 + nfull, :], pt[:kc, nfull * 16:(nfull + 1) * 16], eng="dve")
                po = nxt("psO", psO)
                for j in range(nch):
                    kc = min(128, nkh - j * 128)
                    pg_ = (0 if half == 0 else 64) + j
                    rhs = Vsel[:, pg_, :] if pg_ < 128 else vnew[0:4, 0, :]
                    MM(po[:16, :128], lhsT=ptsh[:kc, j, :], rhs=rhs, start=(j == 0), stop=(j == nch - 1))
                CP(ohs[half][0:16, :], po[:16, :128], eng="act")
            nmn = wkA[0:16, 28:29]
            e0 = wkA[0:16, 29:30]
            e1 = wkA[0:16, 30:31]
            den = wkA[0:16, 31:32]
            TT(nmn, wkA[0:16, 24:25], wkA[0:16, 25:26], ALU.min)
            TT(e0, nmn, wkA[0:16, 24:25], ALU.subtract)
            TT(e1, nmn, wkA[0:16, 25:26], ALU.subtract)
            ACT(e0, e0, AF.Exp)
            ACT(e1, e1, AF.Exp)
            TT(den, e0, wkA[0:16, 26:27], ALU.mult)
            STT(den, e1, wkA[0:16, 27:28], den, ALU.mult, ALU.add)
            RECIP(den, den)
            TT(den, den, gt16[:, 1:2], ALU.mult)
            TT(e0, e0, den, ALU.mult)
            TT(e1, e1, den, ALU.mult)
            STT(o16[0:16, :], ohs[0][0:16, :], e0, o16[0:16, :], ALU.mult, ALU.add)
            STT(o16[0:16, :], ohs[1][0:16, :], e1, o16[0:16, :], ALU.mult, ALU.add)
            for kv in range(2):
                DMA(kwf[:, :, :], bass.AP(tensor=stwin.tensor, offset=kv * 512 + kvh * 128, ap=[[1024, 128], [128 * 1024, 4], [1, 128]]))
                if kv == 0:
                    pq = nxt("psA", psA)
                    for ti in range(4):
                        TR(pq[:, ti * 128:(ti + 1) * 128], kwf[:, ti, :], idf[:, :])
                    CP(kwT[:, 0:512], pq[:, :])
                else:
                    CP(vw[:, :, :], kwf[:, :, :])
            DMA(kwT[:, 512:516], KWT[kvh * 128:(kvh + 1) * 128, S:S + 4])
            bw = bch[0]
            for g in range(4):
                DMA(ssh[4 * g:4 * g + 4, 4096:4096 + 516], tb_ap("wins", kvh * 4 + g, 0, 516, 4, 516))
            chunks = []
            for (c0, n) in ((0, 512), (512, 4)):
                ps = nxt("psA", psA)
                MM(ps[:16, :n], lhsT=qts[:, :], rhs=kwT[:, c0:c0 + n], start=True, stop=True)
                chunks.append((ps[:16, :n], c0, n))
            softmax_pv(16, chunks, 516, ssh[0:16, 4096:4096 + 516],
                       lambda j: ((vw[:, j, :], 128) if j < 4 else (vnew[0:4, 1, :], 4)),
                       tmpO[0:16, :], wkA[0:16, 32:33], wkA[0:16, 33:34], Ssb2, Pb2, PTs2)
            STT(o16[0:16, :], tmpO[0:16, :], gt16[:, 2:3], o16[0:16, :], ALU.mult, ALU.add)
            CP(o16b[0:16, :], o16[0:16, :], eng="act")
            for g in range(4):
                DMA(OM[S:S + 4, (kvh * 4 + g) * 128:(kvh * 4 + g + 1) * 128], o16b[4 * g:4 * g + 4, :])

    nsa_prompt()
    if upto <= 7:
        zt = arb_t[0:4, 0:2048]
        MEMSET(zt, 0.0)
        DMA(OM[S:S + 4, :], zt)
    else:
        nsa_sample()
    mixer_out(w_out_b, HA, HB)
    ffn(1, HB, HA)
    ple(1, HA, HB)
    final_norm(HB)
    return finish(nc, P, st)


def finish(nc, P, st):
    P.finalize(st)
    if os.environ.get('KVERBOSE'):
        print('ops/sems', P.counts, P.nwaits, flush=True)
    with nc.Block() as block:
        P.emit(block)
    st.close()
    return nc


def make_inputs(core, I, consts):
    b = core % 4
    c = core
    f = np.float32
    m = {}
    m["x"] = np.concatenate([I["x_prompt"][b], I["x_sample"][c]], axis=0).astype(f)
    m["pp"] = np.concatenate([I["p_prompt"][:, b], I["p_sample"][:, c]], axis=1).astype(f)
    m["st128"] = I["state_dil_w128"][0, c].reshape(128, 2, D)
    m["st512"] = I["state_dil_w512"][0, c].reshape(512, 2, D)
    m["st2048"] = I["state_dil_w2048"][0, c].reshape(2048, 2, D)
    m["stwin"] = I["state_nsa_win"][0, c].reshape(512, 2, 512)
    m["stconv"] = np.ascontiguousarray(I["state_conv"][:, c])
    m["cache"] = I["cache_nsa_kv"][0].reshape(1280 * 128, 2048)
    if UPTO_DEFAULT < 8:
        m["cache"] = m["cache"][:128]
    m["ptab"] = np.ascontiguousarray(I["page_table"][c:c + 1]).astype(np.int32)
    m["relb"] = I["rel_bias"]
    m["norms"] = np.concatenate([I["norm_mix"][0:1], I["norm_ffn"][0:1], I["norm_ple"][0:1],
                                 I["norm_mix"][1:2], I["norm_ffn"][1:2], I["norm_ple"][1:2],
                                 I["norm_final"][None, :]], axis=0)
    m["w_in_a"] = I["w_in_a"][0]
    m["w_out_a"] = I["w_out_a"][0]
    m["w_in_b"] = I["w_in_b"][0]
    m["w_out_b"] = I["w_out_b"][0]
    m["cmp_pe"] = I["cmp_pe"][0]
    m["cmp_w1"] = I["cmp_w1"][0]
    m["cmp_b1"] = I["cmp_b1"][0]
    m["cmp_w2"] = I["cmp_w2"][0]
    m["cmp_b2"] = I["cmp_b2"][0]
    m["w_ffn_in"] = I["w_ffn_in"]
    m["conv_w"] = I["conv_w"]
    m["conv_b"] = I["conv_b"]
    m["w_ffn_out"] = I["w_ffn_out"]
    m["w_ple_gate"] = I["w_ple_gate"]
    m["w_ple_proj"] = I["w_ple_proj"]
    m.update(consts)
    return {k: np.ascontiguousarray(v) for k, v in m.items()}


def make_consts():
    tab, _ = host_tables()
    A, Bv, As, Bs = sel_tables()
    selm = np.zeros((16, 4), np.float32)
    for g in range(4):
        for t in range(4):
            selm[4 * g + t, t] = 1.0
    return {
        "c_tab": tab,
        "c_idf": np.eye(128, dtype=np.float32),
        "c_idb": np.eye(128, dtype=np.float32).astype(ml_dtypes.bfloat16),
        "c_selA": A, "c_selB": Bv, "c_selAs": As, "c_selBs": Bs,
        "c_rowv": (np.arange(S) >= 31).astype(np.float32)[:, None],
        "c_selm": selm,
        "c_iota": np.arange(128, dtype=np.float32)[:, None],
    }


def assemble(res):
    f = np.float32
    R = res
    y_p = np.stack([R[b]["o_y"][:S] for b in range(4)])
    y_s = np.stack([R[c]["o_y"][S:] for c in range(8)])
    outs = [y_p, y_s]
    for gi, (w, _) in enumerate(DIL):
        kp = min(w, S)
        outs.append(np.stack([R[b]["o_d%dp" % w].reshape(kp, 2, H, DH) for b in range(4)])[None])
        outs.append(np.stack([R[c]["o_d%ds" % w].reshape(w, 2, H, DH) for c in range(8)])[None])
    outs.append(np.stack([R[b]["o_winp"].reshape(512, 2, 4, DH) for b in range(4)])[None])
    outs.append(np.stack([R[c]["o_wins"].reshape(512, 2, 4, DH) for c in range(8)])[None])
    outs.append(np.stack([R[b]["o_convp"] for b in range(4)], axis=1))
    outs.append(np.stack([R[c]["o_convs"] for c in range(8)], axis=1))
    outs.append(np.stack([R[b]["o_kvp"].reshape(S, 4, 4, DH) for b in range(4)])[None])
    outs.append(np.stack([R[c]["o_kvs"].reshape(4, 4, 4, DH) for c in range(8)])[None])
    return tuple(np.ascontiguousarray(o.astype(f)) for o in outs)


def kernel(**inputs):
    I = {k: np.asarray(v) for k, v in inputs.items()}
    nc = build()
    consts = make_consts()
    in_maps = [make_inputs(c, I, consts) for c in range(8)]
    res = run_bass_kernel_spmd(nc, in_maps, core_ids=list(range(8)))
    return assemble(res.results)
```
